# Optimizing a Trainium2 kernel written in Bass

```python
import jax, jax.numpy as jnp
from jax import lax
import numpy as np

D_MODEL = 1024
BATCH = 2
SEQ = 8192
DEPTH = 1

CHUNK = 64

D_MIX = D_MODEL
D_POOL = D_MIX // 2
D_RWKV = D_MIX - D_POOL
POOL_WINDOWS = (2, 4, 8, 16)
N_POOL_GROUPS = len(POOL_WINDOWS)
POOL_GROUP = D_POOL // N_POOL_GROUPS
HEAD_SIZE = 64
N_RWKV_HEADS = D_RWKV // HEAD_SIZE
DECAY_LORA = 64
ICLR_LORA = 64
RWKV_SPLITS = (D_RWKV, DECAY_LORA, D_RWKV, D_RWKV, ICLR_LORA, D_RWKV)
D_RWKV_SEG = sum(RWKV_SPLITS)
N_IN = 2 * D_POOL + D_RWKV_SEG
NORM_EPS = 1e-6
GN_EPS = 64e-5
L2_EPS = 1e-12

kernel_name = "hybrid_pool_rwkv7_adaln_block"


def _rmsnorm(x, g):
    xf = x.astype(jnp.float32)
    y = xf * lax.rsqrt(jnp.mean(xf * xf, axis=-1, keepdims=True) + NORM_EPS)
    return (y * g.astype(jnp.float32)).astype(x.dtype)


def _multiscale_pool_diff(u):
    T = u.shape[1]
    cs = jnp.cumsum(u, axis=1)
    pos = jnp.arange(1, T + 1)
    outs = []
    for g, win in enumerate(POOL_WINDOWS):
        sl = slice(g * POOL_GROUP, (g + 1) * POOL_GROUP)
        csg = cs[..., sl]
        lagged = jnp.pad(csg, ((0, 0), (win, 0), (0, 0)))[:, :T]
        count = jnp.minimum(pos, win).astype(u.dtype)
        mean = (csg - lagged) / count[None, :, None]
        outs.append(mean - u[..., sl])
    return jnp.stack(outs, axis=2)


def _rwkv7_scan(r, decay, k, v, kk, a):
    B, T, H, N = r.shape
    xs = tuple(jnp.moveaxis(t, 1, 0) for t in (r, decay, k, v, -kk, kk * a))

    def step(S, inp):
        r_t, w_t, k_t, v_t, am_t, b_t = inp
        sa = jnp.einsum('bhij,bhj->bhi', S, am_t)
        S = (S * w_t[:, :, None, :]
             + sa[..., None] * b_t[:, :, None, :]
             + v_t[..., None] * k_t[:, :, None, :])
        y = jnp.einsum('bhij,bhj->bhi', S, r_t)
        return S, y

    S0 = jnp.zeros((B, H, N, N), jnp.float32)
    _, ys = lax.scan(step, S0, xs)
    return jnp.moveaxis(ys, 0, 1)


def setup_inputs(seed: int = 0) -> dict:
    key = jax.random.key(seed)
    ks = jax.random.split(key, 24)
    f32 = jnp.float32
    L = DEPTH
    nrm = lambda k, s: jax.random.normal(k, s, f32)
    return {
        "x": nrm(ks[0], (BATCH, SEQ, D_MODEL)),
        "c": nrm(ks[1], (BATCH, D_MODEL)),
        "w_ada": nrm(ks[2], (L, D_MODEL, 3 * D_MODEL)) * (0.5 * D_MODEL ** -0.5),
        "b_ada": nrm(ks[3], (L, 3 * D_MODEL)) * 0.01,
        "norm_g": 1.0 + 0.01 * nrm(ks[4], (L, D_MODEL)),
        "w_in": nrm(ks[5], (L, D_MODEL, N_IN)) * D_MODEL ** -0.5,
        "pool_w": nrm(ks[6], (L, N_POOL_GROUPS, POOL_GROUP, POOL_GROUP)) * POOL_GROUP ** -0.5,
        "pool_scale": 1.0 + 0.02 * nrm(ks[7], (L, D_POOL)),
        "mu_shift": jax.random.uniform(ks[8], (L, D_RWKV_SEG), f32),
        "w0": jax.random.uniform(ks[9], (L, D_RWKV), f32, minval=-6.0, maxval=0.0),
        "w_up": nrm(ks[10], (L, DECAY_LORA, D_RWKV)) * 0.1,
        "a0": nrm(ks[11], (L, D_RWKV)) * 0.1,
        "a_up": nrm(ks[12], (L, ICLR_LORA, D_RWKV)) * (0.5 * ICLR_LORA ** -0.5),
        "k_k": 0.85 + 0.02 * nrm(ks[13], (L, D_RWKV)),
        "k_a": 1.0 + 0.02 * nrm(ks[14], (L, D_RWKV)),
        "r_k": nrm(ks[15], (L, D_RWKV)) * 0.1,
        "ln_w": 1.0 + 0.02 * nrm(ks[16], (L, D_RWKV)),
        "ln_b": nrm(ks[17], (L, D_RWKV)) * 0.01,
        "w_out": nrm(ks[18], (L, D_MIX, D_MODEL)) * D_MIX ** -0.5,
        "final_g": 1.0 + 0.01 * nrm(ks[19], (D_MODEL,)),
    }


def reference(x, c, w_ada, b_ada, norm_g, w_in, pool_w, pool_scale, mu_shift, w0,
              w_up, a0, a_up, k_k, k_a, r_k, ln_w, ln_b, w_out, final_g):
    B, T, _ = x.shape
    H, N = N_RWKV_HEADS, HEAD_SIZE
    f32 = jnp.float32
    split_idx = list(np.cumsum(RWKV_SPLITS)[:-1])
    for l in range(DEPTH):
        mod = c @ w_ada[l] + b_ada[l]
        shift, scale, gate = jnp.split(mod, 3, axis=-1)
        h = _rmsnorm(x, norm_g[l]) * (1.0 + scale[:, None, :]) + shift[:, None, :]

        p = h @ w_in[l]
        pool_u = p[..., :D_POOL]
        pool_z = p[..., D_POOL:2 * D_POOL]
        seg = p[..., 2 * D_POOL:]

        diff = _multiscale_pool_diff(pool_u.astype(f32))
        y_pool = jnp.einsum('btgc,gcd->btgd', diff, pool_w[l].astype(f32))
        y_pool = y_pool.reshape(B, T, D_POOL) * pool_scale[l].astype(f32)
        y_pool = y_pool * jax.nn.silu(pool_z.astype(f32))

        seg = seg.astype(f32)
        prev = jnp.pad(seg, ((0, 0), (1, 0), (0, 0)))[:, :T]
        seg = seg + (prev - seg) * mu_shift[l].astype(f32)
        r, w_lo, k, v, a_lo, z = jnp.split(seg, split_idx, axis=-1)
        w_raw = w0[l] + jnp.tanh(w_lo) @ w_up[l].astype(f32)
        w_raw = -jax.nn.softplus(-w_raw) - 0.5
        decay = jnp.exp(-jnp.exp(w_raw))
        a = jax.nn.sigmoid(a0[l] + a_lo @ a_up[l].astype(f32))
        kk = (k * k_k[l]).reshape(B, T, H, N)
        kk = kk / jnp.maximum(jnp.linalg.norm(kk, axis=-1, keepdims=True), L2_EPS)
        k = k * (1.0 + (a - 1.0) * k_a[l])
        rh, kh, vh = (t.reshape(B, T, H, N) for t in (r, k, v))
        yh = _rwkv7_scan(rh, decay.reshape(B, T, H, N), kh, vh, kk,
                         a.reshape(B, T, H, N))
        mu = jnp.mean(yh, axis=-1, keepdims=True)
        var = jnp.mean(jnp.square(yh - mu), axis=-1, keepdims=True)
        yh = (yh - mu) * lax.rsqrt(var + GN_EPS)
        yh = yh * ln_w[l].reshape(H, N) + ln_b[l].reshape(H, N)
        bonus = jnp.sum(rh * kh * r_k[l].reshape(H, N), axis=-1, keepdims=True) * vh
        y_rwkv = (yh + bonus).reshape(B, T, D_RWKV) * jax.nn.silu(z)

        mix = jnp.concatenate([y_pool, y_rwkv], axis=-1).astype(x.dtype)
        out = mix @ w_out[l]
        x = x + gate[:, None, :] * out
    return _rmsnorm(x, final_g)
```

```python
import contextlib
import os
import numpy as np
import concourse.bass as bass
import concourse.mybir as mybir
from concourse.bass_utils import run_bass_kernel_spmd

F32 = mybir.dt.float32
BF16 = mybir.dt.bfloat16
ALU = mybir.AluOpType
AF = mybir.ActivationFunctionType
AX = mybir.AxisListType

COMPUTE = ("pe", "act", "dve", "pool")
NDMASEM = 12

NT = 2048
HALO = 16
D = 1024
NIN = 3200
NST = 4
NPV = 90
CW = 128 + 64 + 64 + 128 + 128 + 512 + 64
C_M2, C_ML, C_I64, C_I128, C_BO, C_SC, C_INV = 0, 128, 192, 256, 384, 512, 1024
LOGK = -0.60653066 * 0.5


class Buf:
    def __init__(self, t, name, psum=False):
        self.t = t
        self.name = name
        self.psum = psum
        self.base_w = None
        self.base_r = []
        self.reg = {}

    def __getitem__(self, idx):
        return self.t[idx]

    def deps_read(self, key):
        d = []
        if self.base_w is not None:
            d.append(self.base_w)
        if key is None:
            for w, _ in self.reg.values():
                if w is not None:
                    d.append(w)
        else:
            st = self.reg.get(key)
            if st and st[0] is not None:
                d.append(st[0])
        return d

    def deps_write(self, key):
        d = self.deps_read(key)
        d += self.base_r
        if key is None:
            for _, rs in self.reg.values():
                d += rs
        else:
            st = self.reg.get(key)
            if st:
                d += st[1]
        return d

    def note_read(self, key, tok):
        if key is None:
            self.base_r.append(tok)
        else:
            self.reg.setdefault(key, [None, []])[1].append(tok)

    def note_write(self, key, tok):
        if key is None:
            self.base_w = tok
            self.base_r = []
            self.reg = {}
        else:
            self.reg[key] = [tok, []]


class Prog:
    def __init__(self):
        self.ops = {e: [] for e in COMPUTE + ("sp",)}
        self.count = {e: 0 for e in COMPUTE}
        self.waited = {e: {} for e in COMPUTE + ("sp",)}
        self.dma_count = [0] * NDMASEM
        self.dma_rr = 0
        self.ncc = 0
        self.stopped = False
        self.limit = os.environ.get("K_LIMIT", "")

    def checkpoint(self, name):
        if self.limit and name == self.limit:
            self.stopped = True

    def _emit(self, eng, fn, deps, tok_self, extra=()):
        waits = {}
        for (sk, v) in list(deps) + list(extra):
            if sk == eng and eng in COMPUTE:
                if eng == "pe" or v < self.count[eng] - 2:
                    continue
            if v > waits.get(sk, 0):
                waits[sk] = v
        wl = []
        for sk, v in waits.items():
            if self.waited[eng].get(sk, 0) >= v:
                continue
            self.waited[eng][sk] = v
            wl.append((sk, v))
        self.ops[eng].append((wl, fn, tok_self))

    def _deps(self, reads, writes):
        deps = []
        for b, k in reads:
            deps += b.deps_read(k)
        for b, k in writes:
            deps += b.deps_write(k)
        return deps

    def _note(self, reads, writes, tok):
        for b, k in reads:
            b.note_read(k, tok)
        for b, k in writes:
            b.note_write(k, tok)

    def op(self, eng, fn, reads=(), writes=()):
        if self.stopped:
            return None
        pr = [(b, None) for b, k in reads if b.psum]
        if pr:
            reads = [(b, k) for b, k in reads if not b.psum]
            writes = list(writes) + pr
        deps = self._deps(reads, writes)
        self.count[eng] += 1
        tok = (eng, self.count[eng])
        self._emit(eng, fn, deps, tok)
        self._note(reads, writes, tok)
        return tok

    def dma(self, fn, reads=(), writes=(), q="sp"):
        if self.stopped:
            return None
        deps = self._deps(reads, writes)
        i = self.dma_rr
        self.dma_rr = (self.dma_rr + 1) % NDMASEM
        extra = []
        if self.dma_count[i] > 0:
            extra.append((("d", i), self.dma_count[i]))
        self.dma_count[i] += 16
        tok = (("d", i), self.dma_count[i])
        self._emit(q, fn, deps, tok, extra)
        self._note(reads, writes, tok)
        return tok

    def cc(self, fn, reads=(), writes=()):
        if self.stopped:
            return None
        deps = self._deps(reads, writes)
        self.ncc += 1
        tok = (("c", 0), self.ncc)
        self._emit("pool", fn, deps, tok)
        self._note(reads, writes, tok)
        return tok


def build_nc():
    nc = bass.Bass("TRN2", target_bir_lowering=False)
    P = Prog()

    def din(name, shape, dt=F32):
        return nc.dram_tensor(name, shape, dt, kind="ExternalInput").ap()

    x_d = din("x", [NT + HALO, D])
    crep_d = din("crep", [128, 8, 128])
    wada_d = din("w_ada", [D, 3 * D])
    win_d = din("w_in", [D, NIN])
    wout_d = din("w_out", [D, D])
    poolw_d = din("pool_w", [4, 128, 128])
    wup_d = din("wup", [128, 512])
    pv_d = din("pv", [128, NPV])
    bc_d = din("bcast", [128, 2, D])
    cst_d = din("cst", [128, CW])
    out_d = nc.dram_tensor("out", [NT, D], F32, kind="ExternalOutput").ap()
    sp_mixp = nc.dram_tensor("sp_mixp", [NST, 128, 4, 512], BF16, kind="Internal").ap()
    sp_ct = nc.dram_tensor("sp_ct", [NST, 4, 128, 512], BF16, kind="Internal").ap()
    sp_dt = nc.dram_tensor("sp_dt", [NST, 4, 128, 512], BF16, kind="Internal").ap()
    sp_y = nc.dram_tensor("sp_y", [NST, 4, 128, 512], F32, kind="Internal").ap()
    sp_q = nc.dram_tensor("sp_q", [NST, 4, 128, 512], BF16, kind="Internal").ap()
    gin_d = nc.dram_tensor("gin", [512, 128], F32, kind="Internal").ap()
    gout_d = nc.dram_tensor("gout", [4096, 128], F32, kind="Internal").ap()
    dr = {n: Buf(None, n) for n in ("mixp", "ct", "dt", "y", "q", "gin", "gout")}

    es = contextlib.ExitStack()
    with es:
        def sb(name, shape, dt=F32):
            return Buf(es.enter_context(nc.sbuf_tensor("s_" + name, shape, dt)), name)

        def ps(name, shape, dt=F32):
            return Buf(es.enter_context(nc.psum_tensor("p_" + name, shape, dt)), name, psum=True)

        cst = sb("cst", [128, CW])
        pv = sb("pv", [128, NPV])
        bc = sb("bc", [128, 2, D])
        identb = sb("identb", [128, 128], BF16)
        bob = sb("bob", [128, 128], BF16)
        wupb = sb("wupb", [128, 512], BF16)
        poolwb = sb("poolwb", [128, 4, 128], BF16)
        wib = sb("wib", [128, 8, NIN], BF16)
        stg = [sb(f"stg{i}", [128, 640]) for i in range(2)]
        dv = sb("dv", [128, 96])
        DV_SS, DV_GM, DV_BP, DV_OMM, DV_OMKA, DV_HPS, DV_HW0, DV_HA0, DV_HLNW = 0, 16, 24, 49, 66, 70, 74, 78, 82
        PV_NG, PV_BSS, PV_PS, PV_MU, PV_W0, PV_A0, PV_KK, PV_KA, PV_RK, PV_LNW, PV_LNB, PV_HM, PV_SA, PV_SB = \
            0, 8, 24, 28, 45, 49, 53, 57, 61, 65, 69, 73, 74, 82
        carry = sb("carry", [128, 17])
        xt4 = sb("xt4", [128, 4, D])
        crep = xt4
        allst = xt4
        xn = sb("xn", [128, D], BF16)
        ssq = sb("ssq", [128, 8])
        hT = sb("hT", [128, 8, 512], BF16)
        hTh = sb("hTh", [128, 8, 16], BF16)
        ap_ = sb("aprime", [128, 512])
        ucur = sb("ucur", [128, 528])
        uh = sb("uh", [128, 4, 16])
        diffT = sb("diffT", [128, 512], BF16)
        mixp = sb("mixp", [128, 4, 512], BF16)
        twa = sb("twa", [128, 512], BF16)
        r32 = sb("r32", [128, 528])
        k32 = sb("k32", [128, 528])
        v32 = sb("v32", [128, 512])
        z32 = sb("z32", [128, 512])
        wa32 = z32
        pta, ptb = r32, k32
        f = {n: sb(n, [128, 512]) for n in ("lw", "cs", "Ep", "En", "Epp", "Enh", "a32", "kkr", "lnn", "kp", "bb")}
        f["kk"] = f["kkr"]
        f["rn"] = f["lnn"]
        f["g2"] = f["lnn"]
        f["t2"] = f["lw"]
        f["bon"] = f["cs"]
        f["tmpc"] = f["cs"]
        zv, th, sz = f["lw"], f["cs"], f["Ep"]
        sqb = sb("sqb", [128, 512], BF16)
        rkb = sqb
        ctb = sb("ctb", [128, 512], BF16)
        dtb = sb("dtb", [128, 512], BF16)
        arT = sb("arT", [128, 8, 128], BF16)
        bkT = sb("bkT", [128, 8, 128], BF16)
        bhT = sb("bhT", [128, 8, 64], BF16)
        khT = sb("khT", [128, 8, 64], BF16)
        vb = sb("vb", [128, 8, 64], BF16)
        Zt = sb("Zt", [128, 8, 128])
        BKV = sb("BKV", [128, 8, 192], BF16)
        NA = sb("NA", [128, 8, 128], BF16)
        N1 = sb("N1", [128, 8, 64])
        AK = sb("AK", [128, 8, 128], BF16)
        La = sb("La", [128, 8, 64])
        Lb = sb("Lb", [128, 8, 64])
        NPa = sb("NPa", [128, 8, 128])
        NPb = sb("NPb", [128, 8, 128])
        TT = sb("TT", [128, 8, 64])
        WU = sb("WU", [128, 8, 128], BF16)
        MT32 = sb("MT32", [128, 8, 64])
        G32 = sb("G32", [128, 8, 64])
        diagPC = sb("diagPC", [128, 8, 64])
        QhT = sb("QhT", [128, 8, 64], BF16)
        Yloc = diagPC
        yfull = MT32
        qtb = sb("qtb", [128, 8, 64], BF16)
        stb = sb("stb", [128, 8, 128], BF16)
        st32 = [sb(f"st32_{hp}", [128, 128]) for hp in range(4)]
        McT = sb("McT", [128, 4, 64])
        Sig = sb("Sig", [128, 4, 64])
        Sigb = sb("Sigb", [128, 4, 64], BF16)
        ft1 = sb("ft1", [128, 4, 64])
        mixp2 = mixp
        mixr = hT
        yf2 = yfull
        qt2 = qtb
        ct2, dt2 = ctb, dtb
        ycor = Yloc
        cen = G32
        sq2 = v32
        gnb = QhT
        gst = sb("gst", [128, 32])
        tmpm = f["Ep"]
        xo = [f["lw"], f["cs"]]
        ss2 = sb("ss2", [128, 4])

        psA = [ps("psA0", [128, 512]), ps("psA1", [128, 512])]
        psM = ps("psM", [128, 512])
        psT = ps("psT", [128, 1024], BF16)
        psT2 = ps("psT2", [128, 1024], BF16)
        psW = ps("psW", [128, 8, 128])
        psN = ps("psN", [128, 8, 64])

        es.enter_context(nc.cleanup_on_exit())
        sems = {e: nc.alloc_semaphore(name=e) for e in COMPUTE}
        dsems = [nc.alloc_semaphore(name=f"d{i}") for i in range(NDMASEM)]
        ccsem = nc.alloc_semaphore(name="ccs")

        W = lambda b, k=None: (b, k)

        def pvc(c0, n=1):
            return pv[:, c0:c0 + n]

        def dvc(c0, n=1):
            return dv[:, c0:c0 + n]

        crepv = xt4[:, 0, :].rearrange("p (a b) -> p a b", b=128)
        allv = xt4[:, :, :].rearrange("p a (b n) -> p (a b) n", n=128)
        P.dma(lambda e: e.dma_start(out=cst[:], in_=cst_d), writes=[W(cst)])
        P.dma(lambda e: e.dma_start(out=pv[:], in_=pv_d), writes=[W(pv)])
        P.dma(lambda e: e.dma_start(out=crepv, in_=crep_d), writes=[W(xt4)])
        P.dma(lambda e: e.dma_start(out=bc[:], in_=bc_d), writes=[W(bc)], q="act")
        P.dma(lambda e: e.dma_start(out=stg[0][:, 0:512], in_=wup_d), writes=[W(stg[0])], q="act")
        P.op("dve", lambda e: e.tensor_copy(out=wupb[:], in_=stg[0][:, 0:512]), reads=[W(stg[0])], writes=[W(wupb)])
        P.dma(lambda e: e.dma_start(out=stg[1][:, 0:512].rearrange("p (g d) -> p g d", g=4),
                                    in_=poolw_d.rearrange("g c d -> c g d")), writes=[W(stg[1])], q="act")
        P.op("dve", lambda e: e.tensor_copy(out=poolwb[:].rearrange("p g d -> p (g d)"), in_=stg[1][:, 0:512]),
             reads=[W(stg[1])], writes=[W(poolwb)])
        P.op("pool", lambda e: e.tensor_copy(out=identb[:], in_=cst[:, C_I128:C_I128 + 128]), reads=[W(cst)], writes=[W(identb)])
        P.op("pool", lambda e: e.tensor_copy(out=bob[:], in_=cst[:, C_BO:C_BO + 128]), reads=[W(cst)], writes=[W(bob)])
        for hp in range(4):
            P.op("pool", lambda e, hp=hp: e.memset(st32[hp][:, 0:64], 0.0), writes=[W(st32[hp])])
            P.op("pool", lambda e, hp=hp: e.tensor_copy(out=st32[hp][:, 64:128], in_=cst[:, C_I64:C_I64 + 64]),
                 reads=[W(cst), W(st32[hp])], writes=[W(st32[hp])])
        def dscal(dst, src, n, m, a_, key):
            if a_ is None:
                P.op("dve", lambda e: e.tensor_scalar(out=dvc(dst, n), in0=pvc(src, n), scalar1=m, scalar2=None, op0=ALU.mult),
                     reads=[W(pv)], writes=[W(dv, key)])
            else:
                P.op("dve", lambda e: e.tensor_scalar(out=dvc(dst, n), in0=pvc(src, n), scalar1=m, scalar2=a_, op0=ALU.mult,
                                                       op1=ALU.add), reads=[W(pv)], writes=[W(dv, key)])
        dscal(DV_OMM, PV_MU, 17, -1.0, 1.0, "omm")
        dscal(DV_OMKA, PV_KA, 4, -1.0, 1.0, "omka")
        dscal(DV_HPS, PV_PS, 4, 0.5, None, "hps")
        dscal(DV_HW0, PV_W0, 4, 0.5, None, "hw0")
        dscal(DV_HA0, PV_A0, 4, 0.5, None, "ha0")
        dscal(DV_HLNW, PV_LNW, 4, 0.5, None, "hlnw")

        P.op("dve", lambda e: e.memset(psM[:, 0:32], 0.0), writes=[W(psM)])
        nst_ = [0]

        def next_stg():
            b_ = stg[nst_[0] % 2]
            nst_[0] += 1
            return b_
        for dc in range(8):
            for q in range(6):
                s_ = next_stg()
                P.dma(lambda e, dc=dc, q=q, s_=s_: e.dma_start(out=s_[:, 0:512],
                                                               in_=wada_d[dc * 128:(dc + 1) * 128, q * 512:(q + 1) * 512]),
                      writes=[W(s_)], q=("sp" if q % 2 == 0 else "act"))
                if q < 4:
                    for kk_ in range(4):
                        k = q * 4 + kk_
                        P.op("pe", lambda e, dc=dc, k=k, kk_=kk_, s_=s_: e.matmul(
                            psM[:, k:k + 1], lhsT=s_[:, kk_ * 128:(kk_ + 1) * 128], rhs=crepv[:, dc, 0:1], start=False,
                            stop=False, skip_group_check=True), reads=[W(s_), W(xt4)], writes=[W(psM)])
                else:
                    hf = q - 4
                    P.op("pe", lambda e, dc=dc, hf=hf, s_=s_: e.matmul(psA[hf][:, :], lhsT=crepv[:, dc, :], rhs=s_[:, 0:512],
                                                                     start=(dc == 0), stop=(dc == 7)),
                         reads=[W(s_), W(xt4)], writes=[W(psA[hf])])
        P.op("dve", lambda e: e.tensor_tensor(out=dvc(DV_SS, 16), in0=psM[:, 0:16], in1=pvc(PV_BSS, 16), op=ALU.add),
             reads=[W(psM), W(pv)], writes=[W(dv, "ss")])
        P.op("dve", lambda e: e.scalar_tensor_tensor(out=dvc(DV_GM, 8), in0=dvc(DV_SS + 8, 8), scalar=1.0,
                                                      in1=pvc(PV_NG, 8), op0=ALU.add, op1=ALU.mult),
             reads=[W(dv, "ss"), W(pv)], writes=[W(dv, "gm")])
        for hf in range(2):
            P.op("dve", lambda e, hf=hf: e.tensor_tensor(out=bc[:, 0, hf * 512:(hf + 1) * 512], in0=psA[hf][:, :],
                                                         in1=bc[:, 0, hf * 512:(hf + 1) * 512], op=ALU.add),
                 reads=[W(psA[hf]), W(bc)], writes=[W(bc)])

        P.op("dve", lambda e: e.memset(psM[:, 0:32], 0.0), reads=[W(psM)], writes=[W(psM)])
        WIB_R = []
        for dc in range(8):
            for q in range(5):
                s_ = next_stg()
                P.dma(lambda e, dc=dc, q=q, s_=s_: e.dma_start(out=s_[:, 0:640],
                                                               in_=win_d[dc * 128:(dc + 1) * 128, q * 640:(q + 1) * 640]),
                      writes=[W(s_)], q=("sp" if q % 2 == 0 else "act"))
                for kk_ in range(5):
                    cc = q * 5 + kk_
                    P.op("pe", lambda e, dc=dc, cc=cc, kk_=kk_, s_=s_: e.matmul(
                        psM[:, cc:cc + 1], lhsT=s_[:, kk_ * 128:(kk_ + 1) * 128], rhs=dv[:, DV_SS + dc:DV_SS + dc + 1],
                        start=False, stop=False, skip_group_check=True), reads=[W(s_), W(dv, "ss")], writes=[W(psM)])
                eng = ("dve", "pool", "act")[(dc * 5 + q) % 3]
                if eng == "act":
                    P.op("act", lambda e, dc=dc, q=q, s_=s_: e.activation(out=wib[:, dc, q * 640:(q + 1) * 640], in_=s_[:, 0:640],
                                                                         func=AF.Copy, scale=dvc(DV_GM + dc)),
                         reads=[W(s_), W(dv, "gm")], writes=[W(wib, (q, dc))])
                else:
                    P.op(eng, lambda e, dc=dc, q=q, s_=s_: e.tensor_scalar(out=wib[:, dc, q * 640:(q + 1) * 640], in0=s_[:, 0:640],
                                                                          scalar1=dvc(DV_GM + dc), scalar2=None, op0=ALU.mult),
                         reads=[W(s_), W(dv, "gm")], writes=[W(wib, (q, dc))])
                WIB_R.append(W(wib, (q, dc)))
        P.op("dve", lambda e: e.tensor_copy(out=dvc(DV_BP, 25), in_=psM[:, 0:25]), reads=[W(psM)], writes=[W(dv, "bp")])

        P.checkpoint('setup')
        def rms_rstd(src_cols, ncol, out_ap_cols):
            P.op("act", lambda e: e.activation(out=out_ap_cols, in_=src_cols, func=AF.Ln, bias=1e-6, scale=1.0 / D),
                 reads=[W(ssq)], writes=[W(ssq)])
            P.op("act", lambda e: e.activation(out=out_ap_cols, in_=out_ap_cols, func=AF.Exp, scale=-0.5),
                 reads=[W(ssq)], writes=[W(ssq)])

        P.dma(lambda e: e.dma_start(out=xt4[0:16, 0, :], in_=x_d[0:16, :]), writes=[W(xt4)])
        P.op("act", lambda e: e.activation(out=xn[0:16, :], in_=xt4[0:16, 0, :], func=AF.Square, accum_out=ssq[0:16, 0:1]),
             reads=[W(xt4)], writes=[W(xn), W(ssq)])
        rms_rstd(ssq[0:16, 0:1], 1, ssq[0:16, 4:5])
        P.op("dve", lambda e: e.tensor_scalar(out=xn[0:16, :], in0=xt4[0:16, 0, :], scalar1=ssq[0:16, 4:5], scalar2=None,
                                               op0=ALU.mult), reads=[W(xt4), W(ssq)], writes=[W(xn)])
        for dc in range(8):
            P.op("pe", lambda e, dc=dc: e.transpose(out=psT[:, dc * 16:(dc + 1) * 16], in_=xn[0:16, dc * 128:(dc + 1) * 128],
                                                    identity=identb[0:16, 0:16]),
                 reads=[W(xn), W(identb)], writes=[W(psT)])
        P.op("dve", lambda e: e.tensor_copy(out=hTh[:].rearrange("p a b -> p (a b)"), in_=psT[:, 0:128]),
             reads=[W(psT)], writes=[W(hTh)])
        for g in range(4):
            for dc in range(8):
                P.op("pe", lambda e, g=g, dc=dc: e.matmul(psM[:, g * 16:(g + 1) * 16], lhsT=wib[:, dc, g * 128:(g + 1) * 128],
                                                        rhs=hTh[:, dc, :], start=(dc == 0), stop=(dc == 7)),
                     reads=WIB_R + [W(hTh)], writes=[W(psM)])
        for g in range(4):
            P.op("dve", lambda e, g=g: e.tensor_scalar(out=uh[:, g, :], in0=psM[:, g * 16:(g + 1) * 16],
                                                       scalar1=dvc(DV_BP + g), scalar2=pvc(PV_HM), op0=ALU.add, op1=ALU.mult),
                 reads=[W(psM), W(dv, "bp"), W(pv)], writes=[W(uh, g)])
        for sc in range(17):
            for dc in range(8):
                P.op("pe", lambda e, sc=sc, dc=dc: e.matmul(psW[:, :, :].rearrange("p a b -> p (a b)")[:, sc * 16:(sc + 1) * 16],
                                                          lhsT=wib[:, dc, (8 + sc) * 128:(9 + sc) * 128], rhs=hTh[:, dc, :],
                                                          start=(dc == 0), stop=(dc == 7)),
                     reads=WIB_R + [W(hTh)], writes=[W(psW)])
        psWf = psW[:, :, :].rearrange("p a b -> p (a b)")
        P.op("dve", lambda e: e.tensor_tensor(out=carry[:, :], in0=psWf[:, 15:272:16], in1=dvc(DV_BP + 8, 17), op=ALU.add),
             reads=[W(psW), W(dv, "bp")], writes=[W(carry)])
        P.op("dve", lambda e: e.tensor_scalar(out=carry[:, :], in0=carry[:, :], scalar1=pvc(PV_HM), scalar2=None, op0=ALU.mult),
             reads=[W(carry), W(pv)], writes=[W(carry)])
        P.op("dve", lambda e: e.tensor_tensor(out=carry[:, :], in0=carry[:, :], in1=dvc(DV_BP + 8, 17), op=ALU.subtract),
             reads=[W(carry), W(dv, "bp")], writes=[W(carry)])

        P.checkpoint('halo')
        bank = [0]

        def inproj(cc):
            b = psA[bank[0] % 2]
            bank[0] += 1
            for dc in range(8):
                P.op("pe", lambda e, dc=dc, b=b: e.matmul(b[:, :], lhsT=wib[:, dc, cc * 128:(cc + 1) * 128], rhs=hT[:, dc, :],
                                                        start=(dc == 0), stop=(dc == 7)),
                     reads=WIB_R + [W(hT)], writes=[W(b)])
            return b

        def evac_shift(cc, dst):
            sc = cc - 8
            b = inproj(cc)
            P.op("act", lambda e: e.activation(out=ap_[:, :], in_=b[:, :], func=AF.Identity, bias=dvc(DV_BP + cc),
                                               scale=dvc(DV_OMM + sc)),
                 reads=[W(b), W(dv, "bp"), W(dv, "omm")], writes=[W(ap_)])
            P.op("dve", lambda e: e.scalar_tensor_tensor(out=dst[:, 1:512], in0=b[:, 0:511], scalar=pvc(PV_MU + sc),
                                                          in1=ap_[:, 1:512], op0=ALU.mult, op1=ALU.add),
                 reads=[W(b), W(ap_), W(pv)], writes=[W(dst)])
            P.op("dve", lambda e: e.scalar_tensor_tensor(out=dst[:, 0:1], in0=carry[:, sc:sc + 1], scalar=pvc(PV_MU + sc),
                                                          in1=ap_[:, 0:1], op0=ALU.mult, op1=ALU.add),
                 reads=[W(carry, sc), W(ap_), W(pv)], writes=[W(dst)])
            P.op("act", lambda e: e.activation(out=carry[:, sc:sc + 1], in_=b[:, 511:512], func=AF.Copy),
                 reads=[W(b)], writes=[W(carry, sc)])

        def v3(buf, lo=None, hi=None):
            a = buf[:, 0:512].rearrange("p (c t) -> p c t", t=64)
            return a if lo is None else a[:, :, lo:hi]

        for s in range(NST):
            r0 = HALO + s * 512
            P.dma(lambda e, r0=r0: e.dma_start(out=xt4[:, :, :], in_=x_d[r0:r0 + 512, :].rearrange("(a p) d -> p a d", p=128)),
                  writes=[W(xt4)])
            for tt in range(4):
                P.op("act", lambda e, tt=tt: e.activation(out=xn[:, :], in_=xt4[:, tt, :], func=AF.Square,
                                                          accum_out=ssq[:, tt:tt + 1]),
                     reads=[W(xt4)], writes=[W(xn), W(ssq)])
            rms_rstd(ssq[:, 0:4], 4, ssq[:, 4:8])
            for tt in range(4):
                P.op("dve", lambda e, tt=tt: e.tensor_scalar(out=xn[:, :], in0=xt4[:, tt, :], scalar1=ssq[:, 4 + tt:5 + tt],
                                                             scalar2=None, op0=ALU.mult),
                     reads=[W(xt4), W(ssq)], writes=[W(xn)])
                pt = psT if tt % 2 == 0 else psT2
                for dc in range(8):
                    P.op("pe", lambda e, dc=dc, pt=pt: e.transpose(out=pt[:, dc * 128:(dc + 1) * 128],
                                                                   in_=xn[:, dc * 128:(dc + 1) * 128], identity=identb[:, :]),
                         reads=[W(xn), W(identb)], writes=[W(pt)])
                eng = "act" if tt % 2 == 0 else "dve"
                if eng == "act":
                    P.op("act", lambda e, tt=tt, pt=pt: e.activation(out=hT[:, :, tt * 128:(tt + 1) * 128],
                                                                     in_=pt[:, :].rearrange("p (a b) -> p a b", b=128), func=AF.Copy),
                         reads=[W(pt)], writes=[W(hT)])
                else:
                    P.op("dve", lambda e, tt=tt, pt=pt: e.tensor_copy(out=hT[:, :, tt * 128:(tt + 1) * 128],
                                                                      in_=pt[:, :].rearrange("p (a b) -> p a b", b=128)),
                         reads=[W(pt)], writes=[W(hT)])

            P.checkpoint(f'A{s}')
            for g in range(4):
                win = 2 ** (g + 1)
                u = ucur
                b = inproj(g)
                P.op("act", lambda e, g=g, b=b: e.activation(out=ucur[:, 16:528], in_=b[:, :], func=AF.Identity,
                                                             bias=dvc(DV_BP + g), scale=1.0),
                     reads=[W(b), W(dv, "bp")], writes=[W(ucur)])
                P.op("pool", lambda e, g=g: e.tensor_copy(out=ucur[:, 0:16], in_=uh[:, g, :]), reads=[W(uh, g), W(ucur)],
                     writes=[W(ucur)])
                b = inproj(4 + g)
                P.op("act", lambda e, g=g, b=b: e.activation(out=zv[:, :], in_=b[:, :], func=AF.Identity,
                                                             bias=dvc(DV_BP + 4 + g), scale=1.0),
                     reads=[W(b), W(dv, "bp")], writes=[W(zv)])
                P.op("act", lambda e: e.activation(out=th[:, :], in_=zv[:, :], func=AF.Tanh, scale=0.5),
                     reads=[W(zv)], writes=[W(th)])
                P.op("dve", lambda e: e.scalar_tensor_tensor(out=sz[:, :], in0=th[:, :], scalar=1.0, in1=zv[:, :],
                                                               op0=ALU.add, op1=ALU.mult),
                     reads=[W(th), W(zv)], writes=[W(sz)])
                P.op("pool", lambda e, u=u: e.tensor_tensor(out=pta[:, 1:528], in0=u[:, 1:528], in1=u[:, 0:527], op=ALU.add),
                     reads=[W(u)], writes=[W(pta)])
                cur, oth = pta, ptb
                sh = 2
                lo = 1
                while sh < win:
                    lo2 = lo + sh
                    P.op("pool", lambda e, cur=cur, oth=oth, lo2=lo2, sh=sh: e.tensor_tensor(
                        out=oth[:, lo2:528], in0=cur[:, lo2:528], in1=cur[:, lo2 - sh:528 - sh], op=ALU.add),
                        reads=[W(cur)], writes=[W(oth)])
                    cur, oth = oth, cur
                    lo = lo2
                    sh *= 2
                P.op("dve", lambda e, cur=cur, u=u, win=win: e.scalar_tensor_tensor(
                    out=diffT[:, :], in0=cur[:, 16:528], scalar=1.0 / win, in1=u[:, 16:528], op0=ALU.mult, op1=ALU.subtract),
                    reads=[W(cur), W(u)], writes=[W(diffT)])
                if s == 0:
                    P.op("pool", lambda e, cur=cur, oth=oth, g=g: e.tensor_tensor(
                        out=oth[:, 0:16], in0=cur[:, 16:32], in1=cst[:, C_INV + g * 16:C_INV + (g + 1) * 16], op=ALU.mult),
                        reads=[W(cur), W(cst)], writes=[W(oth)])
                    P.op("pool", lambda e, oth=oth, u=u: e.tensor_tensor(out=diffT[:, 0:16], in0=oth[:, 0:16], in1=u[:, 16:32],
                                                                         op=ALU.subtract),
                         reads=[W(oth), W(u), W(diffT)], writes=[W(diffT)])
                P.op("pool", lambda e, u=u, g=g: e.tensor_copy(out=uh[:, g, :], in_=u[:, 512:528]), reads=[W(u)],
                     writes=[W(uh, g)])
                P.op("pe", lambda e, g=g: e.matmul(psM[:, :], lhsT=poolwb[:, g, :], rhs=diffT[:, :], start=True, stop=True),
                     reads=[W(poolwb), W(diffT)], writes=[W(psM)])
                P.op("dve", lambda e, g=g: e.scalar_tensor_tensor(out=mixp[:, g, :], in0=psM[:, :], scalar=dvc(DV_HPS + g),
                                                                  in1=sz[:, :], op0=ALU.mult, op1=ALU.mult),
                     reads=[W(psM), W(sz), W(dv, "hps")], writes=[W(mixp)])
            P.dma(lambda e, s=s: e.dma_start(out=sp_mixp[s], in_=mixp[:, :, :]), reads=[W(mixp)], writes=[W(dr["mixp"], s)])

            P.checkpoint(f'B{s}')
            evac_shift(24, wa32)
            P.op("act", lambda e: e.activation(out=twa[0:64, :], in_=wa32[0:64, :], func=AF.Tanh), reads=[W(wa32)],
                 writes=[W(twa, 0)])
            P.op("pool", lambda e: e.tensor_copy(out=twa[64:128, :], in_=wa32[64:128, :]), reads=[W(wa32)], writes=[W(twa, 1)])
            for hp in range(4):
                evac_shift(8 + hp, r32)
                evac_shift(12 + hp, k32)
                evac_shift(16 + hp, v32)
                evac_shift(20 + hp, z32)
                hs = slice(hp * 128, (hp + 1) * 128)
                P.op("pe", lambda e, hs=hs: e.matmul(psM[:, :], lhsT=wupb[0:64, hs], rhs=twa[0:64, :], start=True, stop=True),
                     reads=[W(wupb), W(twa, 0)], writes=[W(psM)])
                P.op("act", lambda e, hp=hp: e.activation(out=f["lw"][:, :], in_=psM[:, :], func=AF.Tanh, bias=dvc(DV_HW0 + hp),
                                                          scale=0.5), reads=[W(psM), W(dv, "hw0")], writes=[W(f["lw"])])
                P.op("dve", lambda e: e.tensor_scalar(out=f["lw"][:, :], in0=f["lw"][:, :], scalar1=LOGK, scalar2=LOGK,
                                                       op0=ALU.mult, op1=ALU.add), reads=[W(f["lw"])], writes=[W(f["lw"])])
                P.op("dve", lambda e: e.tensor_tensor_scan(out=f["cs"][:, :], data0=cst[:, C_SC:C_SC + 512], data1=f["lw"][:, :],
                                                            initial=0.0, op0=ALU.mult, op1=ALU.add),
                     reads=[W(f["lw"]), W(cst)], writes=[W(f["cs"])])
                P.op("act", lambda e: e.activation(out=f["Ep"][:, :], in_=f["cs"][:, :], func=AF.Exp), reads=[W(f["cs"])],
                     writes=[W(f["Ep"])])
                P.op("act", lambda e: e.activation(out=f["En"][:, :], in_=f["cs"][:, :], func=AF.Exp, scale=-1.0),
                     reads=[W(f["cs"])], writes=[W(f["En"])])
                P.op("pool", lambda e: e.tensor_copy(out=f["Epp"][:, 1:512], in_=f["Ep"][:, 0:511]), reads=[W(f["Ep"])],
                     writes=[W(f["Epp"])])
                P.op("pool", lambda e: e.memset(v3(f["Epp"], 0, 1), 1.0), reads=[W(f["Epp"])], writes=[W(f["Epp"])])
                P.op("pool", lambda e: e.tensor_tensor(out=v3(f["Enh"]), in0=v3(f["En"]),
                                                       in1=v3(f["Ep"], 63, 64).broadcast_to([128, 8, 64]), op=ALU.mult),
                     reads=[W(f["En"]), W(f["Ep"])], writes=[W(f["Enh"])])
                P.op("pool", lambda e: e.tensor_tensor(out=diagPC[:, :, :],
                                                       in0=cst[:, C_I64:C_I64 + 64].unsqueeze(1).broadcast_to([128, 8, 64]),
                                                       in1=v3(f["Ep"], 63, 64).broadcast_to([128, 8, 64]), op=ALU.mult),
                     reads=[W(cst), W(f["Ep"])], writes=[W(diagPC)])
                P.op("pe", lambda e, hs=hs: e.matmul(psM[:, :], lhsT=wupb[64:128, hs], rhs=twa[64:128, :], start=True, stop=True),
                     reads=[W(wupb), W(twa, 1)], writes=[W(psM)])
                P.op("act", lambda e, hp=hp: e.activation(out=f["a32"][:, :], in_=psM[:, :], func=AF.Tanh, bias=dvc(DV_HA0 + hp),
                                                          scale=0.5), reads=[W(psM), W(dv, "ha0")], writes=[W(f["a32"])])
                P.op("pool", lambda e: e.tensor_scalar(out=f["a32"][:, :], in0=f["a32"][:, :], scalar1=0.5, scalar2=0.5,
                                                        op0=ALU.mult, op1=ALU.add), reads=[W(f["a32"])], writes=[W(f["a32"])])
                P.op("dve", lambda e, hp=hp: e.tensor_scalar(out=f["kkr"][:, :], in0=k32[:, 0:512], scalar1=pvc(PV_KK + hp),
                                                             scalar2=None, op0=ALU.mult),
                     reads=[W(k32), W(pv)], writes=[W(f["kkr"])])
                P.op("pool", lambda e: e.tensor_tensor(out=sqb[:, :], in0=f["kkr"][:, :], in1=f["kkr"][:, :], op=ALU.mult),
                     reads=[W(f["kkr"])], writes=[W(sqb)])
                P.op("pe", lambda e: e.matmul(psM[:, :], lhsT=bob[:, :], rhs=sqb[:, :], start=True, stop=True),
                     reads=[W(bob), W(sqb)], writes=[W(psM)])
                P.op("act", lambda e: e.activation(out=f["lnn"][:, :], in_=psM[:, :], func=AF.Ln, bias=1e-24, scale=1.0),
                     reads=[W(psM)], writes=[W(f["lnn"])])
                P.op("act", lambda e: e.activation(out=f["rn"][:, :], in_=f["lnn"][:, :], func=AF.Exp, scale=-0.5),
                     reads=[W(f["lnn"])], writes=[W(f["rn"])])
                P.op("dve", lambda e: e.tensor_tensor(out=f["kk"][:, :], in0=f["kkr"][:, :], in1=f["rn"][:, :], op=ALU.mult),
                     reads=[W(f["kkr"]), W(f["rn"])], writes=[W(f["kk"])])
                P.op("pool", lambda e, hp=hp: e.tensor_scalar(out=f["t2"][:, :], in0=f["a32"][:, :], scalar1=pvc(PV_KA + hp),
                                                              scalar2=dvc(DV_OMKA + hp), op0=ALU.mult, op1=ALU.add),
                     reads=[W(f["a32"]), W(pv), W(dv, "omka")], writes=[W(f["t2"])])
                P.op("dve", lambda e: e.tensor_tensor(out=f["kp"][:, :], in0=k32[:, 0:512], in1=f["t2"][:, :], op=ALU.mult),
                     reads=[W(k32), W(f["t2"])], writes=[W(f["kp"])])
                P.op("pool", lambda e: e.tensor_tensor(out=f["bb"][:, :], in0=f["kk"][:, :], in1=f["a32"][:, :], op=ALU.mult),
                     reads=[W(f["kk"]), W(f["a32"])], writes=[W(f["bb"])])
                P.op("dve", lambda e: e.scalar_tensor_tensor(out=arT[:, :, 0:64], in0=v3(f["kk"]), scalar=-1.0, in1=v3(f["Epp"]),
                                                              op0=ALU.mult, op1=ALU.mult),
                     reads=[W(f["kk"]), W(f["Epp"])], writes=[W(arT, 0)])
                P.op("pool", lambda e: e.tensor_tensor(out=arT[:, :, 64:128], in0=v3(r32), in1=v3(f["Ep"]), op=ALU.mult),
                     reads=[W(r32), W(f["Ep"])], writes=[W(arT, 1)])
                P.op("dve", lambda e: e.tensor_tensor(out=bkT[:, :, 0:64], in0=v3(f["bb"]), in1=v3(f["En"]), op=ALU.mult),
                     reads=[W(f["bb"]), W(f["En"])], writes=[W(bkT, 0)])
                P.op("pool", lambda e: e.tensor_tensor(out=bkT[:, :, 64:128], in0=v3(f["kp"]), in1=v3(f["En"]), op=ALU.mult),
                     reads=[W(f["kp"]), W(f["En"])], writes=[W(bkT, 1)])
                P.op("dve", lambda e: e.tensor_tensor(out=bhT[:, :, :], in0=v3(f["bb"]), in1=v3(f["Enh"]), op=ALU.mult),
                     reads=[W(f["bb"]), W(f["Enh"])], writes=[W(bhT)])
                P.op("pool", lambda e: e.tensor_tensor(out=khT[:, :, :], in0=v3(f["kp"]), in1=v3(f["Enh"]), op=ALU.mult),
                     reads=[W(f["kp"]), W(f["Enh"])], writes=[W(khT)])
                P.op("act", lambda e: e.activation(out=vb[:, :, :], in_=v3(v32), func=AF.Copy), reads=[W(v32)], writes=[W(vb)])
                P.op("dve", lambda e, hp=hp: e.scalar_tensor_tensor(out=rkb[:, :], in0=r32[:, 0:512], scalar=pvc(PV_RK + hp),
                                                                    in1=f["kp"][:, :], op0=ALU.mult, op1=ALU.mult),
                     reads=[W(r32), W(f["kp"]), W(pv)], writes=[W(rkb)])
                P.op("pe", lambda e: e.matmul(psM[:, :], lhsT=bob[:, :], rhs=rkb[:, :], start=True, stop=True),
                     reads=[W(bob), W(rkb)], writes=[W(psM)])
                P.op("act", lambda e: e.activation(out=f["g2"][:, :], in_=z32[:, :], func=AF.Tanh, scale=0.5),
                     reads=[W(z32)], writes=[W(f["g2"])])
                P.op("dve", lambda e: e.scalar_tensor_tensor(out=f["g2"][:, :], in0=f["g2"][:, :], scalar=1.0, in1=z32[:, :],
                                                               op0=ALU.add, op1=ALU.mult),
                     reads=[W(f["g2"]), W(z32)], writes=[W(f["g2"])])
                P.op("dve", lambda e: e.tensor_tensor(out=f["bon"][:, :], in0=psM[:, :], in1=v32[:, :], op=ALU.mult),
                     reads=[W(psM), W(v32)], writes=[W(f["bon"])])
                P.op("dve", lambda e, hp=hp: e.scalar_tensor_tensor(out=f["tmpc"][:, :], in0=f["bon"][:, :],
                                                                    scalar=pvc(PV_LNB + hp), in1=f["g2"][:, :],
                                                                    op0=ALU.add, op1=ALU.mult),
                     reads=[W(f["bon"]), W(f["g2"]), W(pv)], writes=[W(f["tmpc"])])
                P.op("act", lambda e: e.activation(out=ctb[:, :], in_=f["tmpc"][:, :], func=AF.Copy, scale=0.5),
                     reads=[W(f["tmpc"])], writes=[W(ctb)])
                P.op("pool", lambda e, hp=hp: e.tensor_scalar(out=dtb[:, :], in0=f["g2"][:, :], scalar1=dvc(DV_HLNW + hp),
                                                              scalar2=None, op0=ALU.mult),
                     reads=[W(f["g2"]), W(dv, "hlnw")], writes=[W(dtb)])
                P.dma(lambda e, s=s, hp=hp: e.dma_start(out=sp_ct[s, hp], in_=ctb[:, :]), reads=[W(ctb)],
                      writes=[W(dr["ct"], (s, hp))])
                P.dma(lambda e, s=s, hp=hp: e.dma_start(out=sp_dt[s, hp], in_=dtb[:, :]), reads=[W(dtb)],
                      writes=[W(dr["dt"], (s, hp))])

                P.checkpoint(f'C{s}{hp}')
                units = [(c, h) for c in range(8) for h in range(2)]

                def hsl(h):
                    return slice(64 * h, 64 * h + 64)

                for (src, lo, dstp, off) in ((arT, 0, psT, 0), (bhT, 0, psT, 512), (khT, 0, psT2, 0), (vb, 0, psT2, 512)):
                    for c, h in units:
                        P.op("pe", lambda e, src=src, lo=lo, dstp=dstp, off=off, c=c, h=h: e.transpose(
                            out=dstp[hsl(h), off + c * 64:off + (c + 1) * 64], in_=src[hsl(h), c, lo:lo + 64],
                            identity=identb[hsl(h), hsl(h)]),
                            reads=[W(src, 0) if src is arT else W(src), W(identb)], writes=[W(dstp)])
                P.op("act", lambda e: e.activation(out=Zt[:, :, 0:64], in_=psT[:, 0:512].rearrange("p (c t) -> p c t", t=64),
                                                   func=AF.Copy), reads=[W(psT)], writes=[W(Zt, 0)])
                P.op("dve", lambda e: e.tensor_copy(out=BKV[:, :, 0:64], in_=psT[:, 512:1024].rearrange("p (c t) -> p c t", t=64)),
                     reads=[W(psT)], writes=[W(BKV, 0)])
                P.op("act", lambda e: e.activation(out=BKV[:, :, 64:128], in_=psT2[:, 0:512].rearrange("p (c t) -> p c t", t=64),
                                                   func=AF.Copy), reads=[W(psT2)], writes=[W(BKV, 1)])
                P.op("dve", lambda e: e.tensor_copy(out=BKV[:, :, 128:192],
                                                    in_=psT2[:, 512:1024].rearrange("p (c t) -> p c t", t=64)),
                     reads=[W(psT2)], writes=[W(BKV, 2)])

                def mmu(dst, dlo, dhi, lh, llo, lhi, rh, rlo, rhi, rd=(), start=True, stop=True):
                    for c, h in units:
                        P.op("pe", lambda e, c=c, h=h: e.matmul(dst[hsl(h), c, dlo:dhi], lhsT=lh[hsl(h), c, llo:lhi],
                                                              rhs=rh[hsl(h), c, rlo:rhi], start=start, stop=stop),
                             reads=list(rd), writes=[W(dst)])

                m2 = cst[:, C_M2:C_M2 + 128].unsqueeze(1).broadcast_to([128, 8, 128])
                mL = cst[:, C_ML:C_ML + 64].unsqueeze(1).broadcast_to([128, 8, 64])
                mS = cst[:, C_M2:C_M2 + 64].unsqueeze(1).broadcast_to([128, 8, 64])
                idb = cst[:, C_I64:C_I64 + 64].unsqueeze(1).broadcast_to([128, 8, 64])
                mmu(psW, 0, 128, bkT, 0, 64, arT, 0, 128, rd=[W(bkT, 0), W(arT, 0), W(arT, 1)])
                P.op("dve", lambda e: e.tensor_tensor(out=NA[:, :, :], in0=psW[:, :, :], in1=m2, op=ALU.mult),
                     reads=[W(psW), W(cst)], writes=[W(NA)])
                P.op("dve", lambda e: e.tensor_tensor(out=N1[:, :, :], in0=psW[:, :, 0:64], in1=mS, op=ALU.mult),
                     reads=[W(psW), W(cst)], writes=[W(N1)])
                mmu(psW, 0, 128, bkT, 64, 128, arT, 0, 128, rd=[W(bkT, 1), W(arT, 0), W(arT, 1)])
                P.op("dve", lambda e: e.tensor_tensor(out=AK[:, :, :], in0=psW[:, :, :], in1=m2, op=ALU.mult),
                     reads=[W(psW), W(cst)], writes=[W(AK)])
                mmu(psN, 0, 64, arT, 0, 64, bkT, 0, 64, rd=[W(bkT, 0), W(arT, 0)])
                P.op("dve", lambda e: e.tensor_tensor(out=La[:, :, :], in0=psN[:, :, :], in1=mL, op=ALU.mult),
                     reads=[W(psN), W(cst)], writes=[W(La)])
                P.op("pool", lambda e: e.tensor_tensor(out=NPa[:, :, 64:128], in0=N1[:, :, :], in1=idb, op=ALU.add),
                     reads=[W(N1), W(cst)], writes=[W(NPa, 1)])
                mmu(psW, 0, 64, La, 0, 64, N1, 0, 64, rd=[W(La), W(N1)])
                mmu(psN, 0, 64, N1, 0, 64, La, 0, 64, rd=[W(La), W(N1)])
                P.op("act", lambda e: e.activation(out=NPa[:, :, 0:64], in_=psW[:, :, 0:64], func=AF.Copy),
                     reads=[W(psW)], writes=[W(NPa, 0)])
                P.op("dve", lambda e: e.tensor_copy(out=Lb[:, :, :], in_=psN[:, :, :]), reads=[W(psN)], writes=[W(Lb)])
                Lc, Lo, NPc, NPo = Lb, La, NPa, NPb
                for step in (1, 2, 3):
                    mmu(psW, 0, 128, Lc, 0, 64, NPc, 0, 128, rd=[W(Lc), W(NPc, 0), W(NPc, 1)])
                    mmu(psN, 0, 64, NPc, 0, 64, Lc, 0, 64, rd=[W(Lc), W(NPc, 0)])
                    P.op("act", lambda e, NPo=NPo: e.activation(out=NPo[:, :, 0:64], in_=psW[:, :, 0:64], func=AF.Copy),
                         reads=[W(psW)], writes=[W(NPo, 0)])
                    P.op("dve", lambda e, NPo=NPo, NPc=NPc: e.tensor_tensor(out=NPo[:, :, 64:128], in0=psW[:, :, 64:128],
                                                                            in1=NPc[:, :, 64:128], op=ALU.add),
                         reads=[W(psW), W(NPc, 1)], writes=[W(NPo, 1)])
                    P.op("act", lambda e, Lo=Lo: e.activation(out=Lo[:, :, :], in_=psN[:, :, :], func=AF.Copy),
                         reads=[W(psN)], writes=[W(Lo)])
                    Lc, Lo, NPc, NPo = Lo, Lc, NPo, NPc
                mmu(psW, 64, 128, Lc, 0, 64, NPc, 64, 128, rd=[W(Lc), W(NPc, 1)])
                mmu(psN, 0, 64, NPc, 0, 64, Lc, 0, 64, rd=[W(Lc), W(NPc, 0)])
                P.op("dve", lambda e, NPo=NPo, NPc=NPc: e.tensor_tensor(out=NPo[:, :, 64:128], in0=psW[:, :, 64:128],
                                                                        in1=NPc[:, :, 64:128], op=ALU.add),
                     reads=[W(psW), W(NPc, 1)], writes=[W(NPo, 1)])
                P.op("act", lambda e, Lo=Lo: e.activation(out=Lo[:, :, :], in_=psN[:, :, :], func=AF.Copy),
                     reads=[W(psN)], writes=[W(Lo)])
                Lc, Lo, NPc, NPo = Lo, Lc, NPo, NPc
                mmu(psW, 64, 128, Lc, 0, 64, NPc, 64, 128, rd=[W(Lc), W(NPc, 1)])
                P.op("dve", lambda e, NPc=NPc: e.tensor_tensor(out=TT[:, :, :], in0=psW[:, :, 64:128], in1=NPc[:, :, 64:128],
                                                               op=ALU.add), reads=[W(psW), W(NPc, 1)], writes=[W(TT)])
                mmu(psN, 0, 64, AK, 0, 64, BKV, 128, 192, rd=[W(AK), W(BKV, 2)])
                P.op("act", lambda e: e.activation(out=Zt[:, :, 64:128], in_=psN[:, :, :], func=AF.Copy),
                     reads=[W(psN)], writes=[W(Zt, 1)])
                mmu(psW, 0, 128, TT, 0, 64, Zt, 0, 128, rd=[W(TT), W(Zt, 0), W(Zt, 1)])
                P.op("act", lambda e: e.activation(out=WU[:, :, :], in_=psW[:, :, :], func=AF.Copy), reads=[W(psW)],
                     writes=[W(WU)])
                mmu(psN, 0, 64, WU, 0, 64, BKV, 0, 64, rd=[W(WU), W(BKV, 0)])
                P.op("dve", lambda e: e.tensor_tensor(out=MT32[:, :, :], in0=psN[:, :, :], in1=diagPC[:, :, :], op=ALU.add),
                     reads=[W(psN), W(diagPC)], writes=[W(MT32)])
                for c, h in units:
                    P.op("pe", lambda e, c=c, h=h: e.matmul(psN[hsl(h), c, :], lhsT=BKV[hsl(h), c, 0:64],
                                                          rhs=WU[hsl(h), c, 64:128], start=True, stop=False),
                         reads=[W(WU), W(BKV, 0)], writes=[W(psN)])
                    P.op("pe", lambda e, c=c, h=h: e.matmul(psN[hsl(h), c, :], lhsT=BKV[hsl(h), c, 64:128],
                                                          rhs=BKV[hsl(h), c, 128:192], start=False, stop=True),
                         reads=[W(BKV, 1), W(BKV, 2)], writes=[W(psN)])
                P.op("act", lambda e: e.activation(out=G32[:, :, :], in_=psN[:, :, :], func=AF.Copy),
                     reads=[W(psN)], writes=[W(G32)])
                mmu(psN, 0, 64, WU, 0, 64, NA, 64, 128, rd=[W(WU), W(NA)])
                P.op("dve", lambda e: e.tensor_tensor(out=QhT[:, :, :], in0=psN[:, :, :], in1=arT[:, :, 64:128], op=ALU.add),
                     reads=[W(psN), W(arT, 1)], writes=[W(QhT)])
                for c, h in units:
                    P.op("pe", lambda e, c=c, h=h: e.matmul(psN[hsl(h), c, :], lhsT=NA[hsl(h), c, 64:128],
                                                          rhs=WU[hsl(h), c, 64:128], start=True, stop=False),
                         reads=[W(WU), W(NA)], writes=[W(psN)])
                    P.op("pe", lambda e, c=c, h=h: e.matmul(psN[hsl(h), c, :], lhsT=AK[hsl(h), c, 64:128],
                                                          rhs=BKV[hsl(h), c, 128:192], start=False, stop=True),
                         reads=[W(AK), W(BKV, 2)], writes=[W(psN)])
                P.op("act", lambda e: e.activation(out=Yloc[:, :, :], in_=psN[:, :, :], func=AF.Copy),
                     reads=[W(psN)], writes=[W(Yloc)])
                stt_ = st32[hp]
                for c in range(8):
                    P.op("pool", lambda e, c=c, stt_=stt_: e.tensor_copy(out=stb[:, c, :], in_=stt_[:, :]),
                         reads=[W(stt_)], writes=[W(stb, c)])
                    for h in range(2):
                        P.op("pe", lambda e, c=c, h=h, stt_=stt_: e.matmul(psM[hsl(h), 0:128], lhsT=MT32[hsl(h), c, :],
                                                                         rhs=stt_[hsl(h), :], start=True, stop=True),
                             reads=[W(MT32), W(stt_)], writes=[W(psM)])
                    P.op("dve", lambda e, c=c, stt_=stt_: e.tensor_tensor(out=stt_[:, 0:64], in0=psM[:, 0:64], in1=G32[:, c, :],
                                                                          op=ALU.add),
                         reads=[W(psM), W(G32)], writes=[W(stt_)])
                    P.op("act", lambda e, c=c, stt_=stt_: e.activation(out=stt_[:, 64:128], in_=psM[:, 64:128], func=AF.Copy),
                         reads=[W(psM), W(stt_)], writes=[W(stt_)])
                mmu(psN, 0, 64, QhT, 0, 64, stb, 0, 64, rd=[W(QhT)] + [W(stb, c) for c in range(8)])
                P.op("dve", lambda e: e.tensor_tensor(out=yfull[:, :, :], in0=psN[:, :, :], in1=Yloc[:, :, :], op=ALU.add),
                     reads=[W(psN), W(Yloc)], writes=[W(yfull)])
                mmu(psN, 0, 64, stb, 64, 128, QhT, 0, 64, rd=[W(QhT)] + [W(stb, c) for c in range(8)])
                P.op("act", lambda e: e.activation(out=qtb[:, :, :], in_=psN[:, :, :], func=AF.Copy), reads=[W(psN)],
                     writes=[W(qtb)])
                P.dma(lambda e, s=s, hp=hp: e.dma_start(out=sp_y[s, hp], in_=yfull[:, :, :].rearrange("p c t -> p (c t)")),
                      reads=[W(yfull)], writes=[W(dr["y"], (s, hp))])
                P.dma(lambda e, s=s, hp=hp: e.dma_start(out=sp_q[s, hp], in_=qtb[:, :, :].rearrange("p c t -> p (c t)")),
                      reads=[W(qtb)], writes=[W(dr["q"], (s, hp))])

        P.checkpoint('p1')
        for hp in range(4):
            P.dma(lambda e, hp=hp: e.dma_start(out=gin_d[hp * 128:(hp + 1) * 128, :], in_=st32[hp][:, :]),
                  reads=[W(st32[hp])], writes=[W(dr["gin"], hp)], q="pool")
        P.cc(lambda e: e.collective_compute("AllGather", ALU.bypass, replica_groups=[list(range(8))],
                                            ins=[gin_d.opt()], outs=[gout_d.opt()]),
             reads=[W(dr["gin"])], writes=[W(dr["gout"])])
        P.dma(lambda e: e.dma_start(out=allv, in_=gout_d.rearrange("(rh p) n -> p rh n", p=128)),
              reads=[W(dr["gout"])], writes=[W(xt4)], q="pool")
        for kc in range(8):
            for q in range(2):
                s_ = next_stg()
                P.dma(lambda e, kc=kc, q=q, s_=s_: e.dma_start(out=s_[:, 0:512],
                                                               in_=wout_d[kc * 128:(kc + 1) * 128, q * 512:(q + 1) * 512]),
                      writes=[W(s_)], q=("sp" if q % 2 == 0 else "act"))
                P.op("pool", lambda e, kc=kc, q=q, s_=s_: e.tensor_copy(out=wib[:, kc, q * 512:(q + 1) * 512], in_=s_[:, 0:512]),
                     reads=[W(s_)], writes=[W(wib)])
        P.op("dve", lambda e: e.memset(Sig[:, :, :], 0.0), writes=[W(Sig)])
        hsl2 = lambda h: slice(64 * h, 64 * h + 64)
        for r in range(8):
            for hp in range(4):
                for h in range(2):
                    P.op("pe", lambda e, r=r, hp=hp, h=h: e.matmul(psN[hsl2(h), hp, :], lhsT=allv[hsl2(h), r * 4 + hp, 64:128],
                                                                 rhs=cst[hsl2(h), C_I64:C_I64 + 64], start=True, stop=True),
                         reads=[W(xt4), W(cst)], writes=[W(psN)])
            P.op("act", lambda e: e.activation(out=McT[:, :, :], in_=psN[:, 0:4, :], func=AF.Copy), reads=[W(psN)],
                 writes=[W(McT)])
            for hp in range(4):
                for h in range(2):
                    P.op("pe", lambda e, hp=hp, h=h: e.matmul(psM[hsl2(h), hp * 64:(hp + 1) * 64], lhsT=McT[hsl2(h), hp, :],
                                                            rhs=Sig[hsl2(h), hp, :], start=True, stop=True),
                         reads=[W(McT), W(Sig)], writes=[W(psM)])
            P.op("dve", lambda e, r=r: e.tensor_tensor(out=ft1[:, :, :], in0=psM[:, 0:256].rearrange("p (a b) -> p a b", b=64),
                                                       in1=allv[:, r * 4:r * 4 + 4, 0:64], op=ALU.add),
                 reads=[W(psM), W(xt4)], writes=[W(ft1)])
            P.op("dve", lambda e, r=r: e.tensor_scalar(out=ft1[:, :, :], in0=ft1[:, :, :], scalar1=pvc(PV_SA + r), scalar2=None,
                                                       op0=ALU.mult), reads=[W(ft1), W(pv)], writes=[W(ft1)])
            P.op("dve", lambda e, r=r: e.scalar_tensor_tensor(out=Sig[:, :, :], in0=Sig[:, :, :], scalar=pvc(PV_SB + r),
                                                              in1=ft1[:, :, :], op0=ALU.mult, op1=ALU.add),
                 reads=[W(Sig), W(ft1), W(pv)], writes=[W(Sig)])
        P.op("dve", lambda e: e.tensor_copy(out=Sigb[:, :, :], in_=Sig[:, :, :]), reads=[W(Sig)], writes=[W(Sigb)])

        P.checkpoint('xchg')
        out_toks = []
        for s in range(NST):
            P.dma(lambda e, s=s: e.dma_start(out=mixp2[:, :, :], in_=sp_mixp[s]), reads=[W(dr["mixp"], s)], writes=[W(mixp2)])
            r0 = HALO + s * 512
            P.dma(lambda e, r0=r0: e.dma_start(out=xt4[:, :, :], in_=x_d[r0:r0 + 512, :].rearrange("(a p) d -> p a d", p=128)),
                  writes=[W(xt4)], q="act")
            for hp in range(4):
                P.dma(lambda e, s=s, hp=hp: e.dma_start(out=yf2[:, :, :].rearrange("p c t -> p (c t)"), in_=sp_y[s, hp]),
                      reads=[W(dr["y"], (s, hp))], writes=[W(yf2)])
                P.dma(lambda e, s=s, hp=hp: e.dma_start(out=qt2[:, :, :].rearrange("p c t -> p (c t)"), in_=sp_q[s, hp]),
                      reads=[W(dr["q"], (s, hp))], writes=[W(qt2)], q="act")
                P.dma(lambda e, s=s, hp=hp: e.dma_start(out=ct2[:, :], in_=sp_ct[s, hp]), reads=[W(dr["ct"], (s, hp))],
                      writes=[W(ct2)])
                P.dma(lambda e, s=s, hp=hp: e.dma_start(out=dt2[:, :], in_=sp_dt[s, hp]), reads=[W(dr["dt"], (s, hp))],
                      writes=[W(dt2)], q="act")
                for c in range(8):
                    for h in range(2):
                        P.op("pe", lambda e, c=c, h=h, hp=hp: e.matmul(psN[hsl2(h), c, :], lhsT=qt2[hsl2(h), c, :],
                                                                     rhs=Sigb[hsl2(h), hp, :], start=True, stop=True),
                             reads=[W(qt2), W(Sigb)], writes=[W(psN)])
                P.op("dve", lambda e: e.tensor_tensor(out=ycor[:, :, :], in0=psN[:, :, :], in1=yf2[:, :, :], op=ALU.add),
                     reads=[W(psN), W(yf2)], writes=[W(ycor)])
                P.op("dve", lambda e: e.tensor_reduce(out=gst[:, 0:8], in_=ycor[:, :, :], axis=AX.X, op=ALU.add),
                     reads=[W(ycor)], writes=[W(gst, 0)])
                P.op("dve", lambda e: e.tensor_scalar(out=gst[:, 8:16], in0=gst[:, 0:8], scalar1=1.0 / 64, scalar2=None,
                                                       op0=ALU.mult), reads=[W(gst, 0)], writes=[W(gst, 1)])
                P.op("pool", lambda e: e.tensor_tensor(out=cen[:, :, :], in0=ycor[:, :, :],
                                                       in1=gst[:, 8:16].unsqueeze(2).broadcast_to([128, 8, 64]), op=ALU.subtract),
                     reads=[W(ycor), W(gst, 1)], writes=[W(cen)])
                P.op("pool", lambda e: e.tensor_tensor(out=v3(v32), in0=cen[:, :, :], in1=cen[:, :, :], op=ALU.mult),
                     reads=[W(cen)], writes=[W(sq2)])
                P.op("dve", lambda e: e.tensor_reduce(out=gst[:, 16:24], in_=v3(v32), axis=AX.X, op=ALU.add),
                     reads=[W(sq2)], writes=[W(gst, 2)])
                P.op("act", lambda e: e.activation(out=gst[:, 24:32], in_=gst[:, 16:24], func=AF.Ln, bias=64e-5, scale=1.0 / 64),
                     reads=[W(gst, 2)], writes=[W(gst, 3)])
                P.op("act", lambda e: e.activation(out=gst[:, 24:32], in_=gst[:, 24:32], func=AF.Exp, scale=-0.5),
                     reads=[W(gst, 3)], writes=[W(gst, 3)])
                P.op("dve", lambda e: e.tensor_tensor(out=gnb[:, :, :], in0=cen[:, :, :],
                                                      in1=gst[:, 24:32].unsqueeze(2).broadcast_to([128, 8, 64]), op=ALU.mult),
                     reads=[W(cen), W(gst, 3)], writes=[W(gnb)])
                for c in range(8):
                    for h in range(2):
                        P.op("pe", lambda e, c=c, h=h: e.transpose(out=psT[hsl2(h), c * 64:(c + 1) * 64], in_=gnb[hsl2(h), c, :],
                                                                   identity=identb[hsl2(h), hsl2(h)]),
                             reads=[W(gnb), W(identb)], writes=[W(psT)])
                P.op("dve", lambda e: e.tensor_tensor(out=tmpm[:, :], in0=psT[:, 0:512], in1=dt2[:, :], op=ALU.mult),
                     reads=[W(psT), W(dt2)], writes=[W(tmpm)])
                P.op("pool", lambda e, hp=hp: e.tensor_tensor(out=mixr[:, hp, :], in0=tmpm[:, :], in1=ct2[:, :], op=ALU.add),
                     reads=[W(tmpm), W(ct2)], writes=[W(hT, hp)])
            for tt in range(4):
                ts_ = slice(tt * 128, (tt + 1) * 128)
                for hf in range(2):
                    for kc in range(8):
                        if kc < 4:
                            lh = mixp2[:, kc, ts_]
                        else:
                            lh = hT[:, kc - 4, ts_]
                        P.op("pe", lambda e, kc=kc, hf=hf, lh=lh: e.matmul(
                            psA[hf][:, :], lhsT=lh, rhs=wib[:, kc, hf * 512:(hf + 1) * 512],
                            start=(kc == 0), stop=(kc == 7)),
                            reads=[W(mixp2), W(wib)] + [W(hT, i) for i in range(4)], writes=[W(psA[hf])])
                    P.op("dve", lambda e, hf=hf: e.tensor_tensor(out=xo[hf][:, :], in0=psA[hf][:, :],
                                                                 in1=bc[:, 0, hf * 512:(hf + 1) * 512], op=ALU.mult),
                         reads=[W(psA[hf]), W(bc)], writes=[W(xo[hf])])
                    P.op("pool", lambda e, tt=tt, hf=hf: e.tensor_tensor(out=xt4[:, tt, hf * 512:(hf + 1) * 512], in0=xo[hf][:, :],
                                                                         in1=xt4[:, tt, hf * 512:(hf + 1) * 512], op=ALU.add),
                         reads=[W(xo[hf]), W(xt4)], writes=[W(xt4)])
                P.op("act", lambda e, tt=tt: e.activation(out=xn[:, :], in_=xt4[:, tt, :], func=AF.Square, accum_out=ss2[:, 0:1]),
                     reads=[W(xt4)], writes=[W(xn), W(ss2)])
                P.op("act", lambda e: e.activation(out=ss2[:, 1:2], in_=ss2[:, 0:1], func=AF.Ln, bias=1e-6, scale=1.0 / D),
                     reads=[W(ss2)], writes=[W(ss2)])
                P.op("act", lambda e: e.activation(out=ss2[:, 1:2], in_=ss2[:, 1:2], func=AF.Exp, scale=-0.5),
                     reads=[W(ss2)], writes=[W(ss2)])
                P.op("dve", lambda e, tt=tt: e.scalar_tensor_tensor(out=xt4[:, tt, :], in0=xt4[:, tt, :], scalar=ss2[:, 1:2],
                                                                    in1=bc[:, 1, :], op0=ALU.mult, op1=ALU.mult),
                     reads=[W(xt4), W(ss2), W(bc)], writes=[W(xt4)])
                ro = s * 512 + tt * 128
                out_toks.append(P.dma(lambda e, ro=ro, tt=tt: e.dma_start(out=out_d[ro:ro + 128, :], in_=xt4[:, tt, :]),
                                      reads=[W(xt4)]))
        fin = {}
        for sk, v in [t_ for t_ in out_toks if t_ is not None]:
            fin[sk] = max(fin.get(sk, 0), v)
        P.ops["sp"].append((list(fin.items()), None, None))

        def semof(sk):
            if isinstance(sk, str):
                return sems[sk]
            return dsems[sk[1]] if sk[0] == "d" else ccsem

        def mk(name):
            def fn(e):
                for wl, f_, tok in P.ops[name]:
                    for sk, v in wl:
                        e.wait_ge(semof(sk), v)
                    if f_ is None:
                        continue
                    ins = f_(e)
                    sk, v = tok
                    if isinstance(sk, str):
                        ins.then_inc(sems[sk], 1)
                    elif sk[0] == "d":
                        ins.then_inc(dsems[sk[1]], 16)
                    else:
                        ins.then_inc(ccsem)
            return fn

        with nc.Block() as block:
            block.sync(mk("sp"))
            block.tensor(mk("pe"))
            block.scalar(mk("act"))
            block.vector(mk("dve"))
            block.gpsimd(mk("pool"))
    return nc


_NC_CACHE = {}


def _consts():
    c = np.zeros((128, CW), np.float32)
    p = np.arange(128)
    s = (p % 64)[:, None]
    t = np.arange(64)[None, :]
    c[:, C_M2:C_M2 + 64] = (s < t)
    c[:, C_M2 + 64:C_M2 + 128] = (s <= t)
    c[:, C_ML:C_ML + 64] = (t < s)
    c[:, C_I64:C_I64 + 64] = (s == t)
    c[:, C_I128:C_I128 + 128] = np.eye(128)
    c[:, C_BO:C_BO + 128] = (p[:, None] // 64 == p[None, :] // 64)
    sm = np.ones(512, np.float32)
    sm[::64] = 0
    c[:, C_SC:C_SC + 512] = sm[None, :]
    return c


def kernel(x, c, w_ada, b_ada, norm_g, w_in, pool_w, pool_scale, mu_shift, w0, w_up, a0, a_up, k_k, k_a, r_k,
           ln_w, ln_b, w_out, final_g):
    f32 = np.float32
    x = np.asarray(x, f32)
    B, T, _ = x.shape
    seg = 1024
    perm_seg = np.concatenate([np.arange(0, 512), np.arange(576, 1088), np.arange(1088, 1600), np.arange(1664, 2176),
                               np.arange(512, 576), np.arange(1600, 1664)])
    perm = np.concatenate([np.arange(0, 1024), seg + perm_seg])
    w_in_p = np.ascontiguousarray(np.asarray(w_in, f32)[0][:, perm])
    mu_p = np.asarray(mu_shift, f32)[0][perm_seg]

    def col(v, n):
        return np.ascontiguousarray(np.asarray(v, f32).reshape(n, 128).T)

    wup = np.concatenate([np.asarray(w_up, f32)[0], np.asarray(a_up, f32)[0]], axis=0)
    cst0 = _consts()
    b_ada0 = np.asarray(b_ada, f32)[0]
    bcast = np.zeros((128, 2, D), f32)
    bcast[:, 0, :] = b_ada0[2 * D:][None, :]
    bcast[:, 1, :] = np.asarray(final_g, f32)[None, :]

    in_maps = []
    for cid in range(8):
        b, j = cid // 4, cid % 4
        t0 = j * NT
        xs = np.zeros((NT + HALO, D), f32)
        xs[HALO:] = x[b, t0:t0 + NT]
        if j > 0:
            xs[:HALO] = x[b, t0 - HALO:t0]
        pvv = np.zeros((128, NPV), f32)
        pvv[:, 0:8] = col(np.asarray(norm_g)[0], 8)
        pvv[:, 8:24] = col(b_ada0[:2 * D], 16)
        pvv[:, 24:28] = col(np.asarray(pool_scale)[0], 4)
        pvv[:, 28:45] = col(mu_p, 17)
        pvv[:, 45:49] = col(np.asarray(w0)[0], 4)
        pvv[:, 49:53] = col(np.asarray(a0)[0], 4)
        pvv[:, 53:57] = col(np.asarray(k_k)[0], 4)
        pvv[:, 57:61] = col(np.asarray(k_a)[0], 4)
        pvv[:, 61:65] = col(np.asarray(r_k)[0], 4)
        pvv[:, 65:69] = col(np.asarray(ln_w)[0], 4)
        pvv[:, 69:73] = col(np.asarray(ln_b)[0], 4)
        pvv[:, 73] = 1.0 if j > 0 else 0.0
        for r in range(8):
            sel = 1.0 if (r // 4 == b and r % 4 < j) else 0.0
            pvv[:, 74 + r] = sel
            pvv[:, 82 + r] = 1.0 - sel
        cst = cst0.copy()
        for g in range(4):
            win = 2 ** (g + 1)
            pos = np.arange(1, 17)
            cnt = np.minimum(pos, win) if j == 0 else np.full(16, win)
            cst[:, C_INV + g * 16:C_INV + (g + 1) * 16] = (1.0 / cnt)[None, :]
        crep = np.ascontiguousarray(np.broadcast_to(np.asarray(c, f32)[b].reshape(8, 128).T[:, :, None], (128, 8, 128)))
        in_maps.append({
            "x": xs, "crep": crep, "w_ada": np.ascontiguousarray(np.asarray(w_ada, f32)[0]), "w_in": w_in_p,
            "w_out": np.ascontiguousarray(np.asarray(w_out, f32)[0]), "pool_w": np.ascontiguousarray(np.asarray(pool_w, f32)[0]),
            "wup": np.ascontiguousarray(wup), "pv": pvv, "bcast": bcast, "cst": cst,
        })
    if "nc" not in _NC_CACHE:
        _NC_CACHE["nc"] = build_nc()
    res = run_bass_kernel_spmd(_NC_CACHE["nc"], in_maps, core_ids=list(range(8)))
    out = np.zeros((B, T, D), f32)
    for cid in range(8):
        b, j = cid // 4, cid % 4
        out[b, j * NT:(j + 1) * NT] = res.results[cid]["out"]
    return out
```

```python
import contextlib
import os
import numpy as np
import concourse.bass as bass
import concourse.mybir as mybir
from concourse.bass_utils import run_bass_kernel_spmd

F32 = mybir.dt.float32
BF16 = mybir.dt.bfloat16
ALU = mybir.AluOpType
AF = mybir.ActivationFunctionType
AX = mybir.AxisListType

COMPUTE = ("pe", "act", "dve", "pool")
NDMASEM = 12

NT = 2048
HALO = 16
D = 1024
NIN = 3200
NST = 4
NPV = 90
CW = 128 + 64 + 64 + 128 + 128 + 512 + 64
C_M2, C_ML, C_I64, C_I128, C_BO, C_SC, C_INV = 0, 128, 192, 256, 384, 512, 1024
LOGK = -0.60653066 * 0.5


class Buf:
    def __init__(self, t, name, psum=False):
        self.t = t
        self.name = name
        self.psum = psum
        self.base_w = None
        self.base_r = []
        self.reg = {}

    def __getitem__(self, idx):
        return self.t[idx]

    def deps_read(self, key):
        d = []
        if self.base_w is not None:
            d.append(self.base_w)
        if key is None:
            for w, _ in self.reg.values():
                if w is not None:
                    d.append(w)
        else:
            st = self.reg.get(key)
            if st and st[0] is not None:
                d.append(st[0])
        return d

    def deps_write(self, key):
        d = self.deps_read(key)
        d += self.base_r
        if key is None:
            for _, rs in self.reg.values():
                d += rs
        else:
            st = self.reg.get(key)
            if st:
                d += st[1]
        return d

    def note_read(self, key, tok):
        if key is None:
            self.base_r.append(tok)
        else:
            self.reg.setdefault(key, [None, []])[1].append(tok)

    def note_write(self, key, tok):
        if key is None:
            self.base_w = tok
            self.base_r = []
            self.reg = {}
        else:
            self.reg[key] = [tok, []]


class Prog:
    def __init__(self):
        self.ops = {e: [] for e in COMPUTE + ("sp",)}
        self.count = {e: 0 for e in COMPUTE}
        self.waited = {e: {} for e in COMPUTE + ("sp",)}
        self.dma_count = [0] * NDMASEM
        self.dma_rr = 0
        self.ncc = 0
        self.stopped = False
        self.limit = os.environ.get("K_LIMIT", "")

    def checkpoint(self, name):
        if self.limit and name == self.limit:
            self.stopped = True

    def _emit(self, eng, fn, deps, tok_self, extra=()):
        waits = {}
        for (sk, v) in list(deps) + list(extra):
            if sk == eng and eng in COMPUTE:
                if eng == "pe" or v < self.count[eng] - 2:
                    continue
            if v > waits.get(sk, 0):
                waits[sk] = v
        wl = []
        for sk, v in waits.items():
            if self.waited[eng].get(sk, 0) >= v:
                continue
            self.waited[eng][sk] = v
            wl.append((sk, v))
        self.ops[eng].append((wl, fn, tok_self))

    def _deps(self, reads, writes):
        deps = []
        for b, k in reads:
            deps += b.deps_read(k)
        for b, k in writes:
            deps += b.deps_write(k)
        return deps

    def _note(self, reads, writes, tok):
        for b, k in reads:
            b.note_read(k, tok)
        for b, k in writes:
            b.note_write(k, tok)

    def op(self, eng, fn, reads=(), writes=()):
        if self.stopped:
            return None
        pr = [(b, None) for b, k in reads if b.psum]
        if pr:
            reads = [(b, k) for b, k in reads if not b.psum]
            writes = list(writes) + pr
        deps = self._deps(reads, writes)
        self.count[eng] += 1
        tok = (eng, self.count[eng])
        self._emit(eng, fn, deps, tok)
        self._note(reads, writes, tok)
        return tok

    def dma(self, fn, reads=(), writes=(), q="sp"):
        if self.stopped:
            return None
        deps = self._deps(reads, writes)
        i = self.dma_rr
        self.dma_rr = (self.dma_rr + 1) % NDMASEM
        extra = []
        if self.dma_count[i] > 0:
            extra.append((("d", i), self.dma_count[i]))
        self.dma_count[i] += 16
        tok = (("d", i), self.dma_count[i])
        self._emit(q, fn, deps, tok, extra)
        self._note(reads, writes, tok)
        return tok

    def cc(self, fn, reads=(), writes=()):
        if self.stopped:
            return None
        deps = self._deps(reads, writes)
        self.ncc += 1
        tok = (("c", 0), self.ncc)
        self._emit("pool", fn, deps, tok)
        self._note(reads, writes, tok)
        return tok


def build_nc():
    nc = bass.Bass("TRN2", target_bir_lowering=False)
    P = Prog()

    def din(name, shape, dt=F32):
        return nc.dram_tensor(name, shape, dt, kind="ExternalInput").ap()

    x_d = din("x", [NT + HALO, D])
    crep_d = din("crep", [128, 8, 128])
    wada_d = din("w_ada", [D, 3 * D])
    win_d = din("w_in", [D, NIN])
    wout_d = din("w_out", [D, D])
    poolw_d = din("pool_w", [4, 128, 128])
    wup_d = din("wup", [128, 512])
    pv_d = din("pv", [128, NPV])
    bc_d = din("bcast", [128, 2, D])
    cst_d = din("cst", [128, CW])
    out_d = nc.dram_tensor("out", [NT, D], F32, kind="ExternalOutput").ap()
    sp_mixp = nc.dram_tensor("sp_mixp", [NST, 128, 4, 512], BF16, kind="Internal").ap()
    sp_ct = nc.dram_tensor("sp_ct", [NST, 4, 128, 512], BF16, kind="Internal").ap()
    sp_dt = nc.dram_tensor("sp_dt", [NST, 4, 128, 512], BF16, kind="Internal").ap()
    sp_y = nc.dram_tensor("sp_y", [NST, 4, 128, 512], F32, kind="Internal").ap()
    sp_q = nc.dram_tensor("sp_q", [NST, 4, 128, 512], BF16, kind="Internal").ap()
    gin_d = nc.dram_tensor("gin", [512, 128], F32, kind="Internal").ap()
    gout_d = nc.dram_tensor("gout", [4096, 128], F32, kind="Internal").ap()
    dr = {n: Buf(None, n) for n in ("mixp", "ct", "dt", "y", "q", "gin", "gout")}

    es = contextlib.ExitStack()
    with es:
        def sb(name, shape, dt=F32):
            return Buf(es.enter_context(nc.sbuf_tensor("s_" + name, shape, dt)), name)

        def ps(name, shape, dt=F32):
            return Buf(es.enter_context(nc.psum_tensor("p_" + name, shape, dt)), name, psum=True)

        cst = sb("cst", [128, CW])
        pv = sb("pv", [128, NPV])
        bc = sb("bc", [128, 2, D])
        identb = sb("identb", [128, 128], BF16)
        bob = sb("bob", [128, 128], BF16)
        wupb = sb("wupb", [128, 512], BF16)
        poolwb = sb("poolwb", [128, 4, 128], BF16)
        wib = sb("wib", [128, 8, NIN], BF16)
        stg = [sb(f"stg{i}", [128, 640]) for i in range(2)]
        dv = sb("dv", [128, 96])
        DV_SS, DV_GM, DV_BP, DV_OMM, DV_OMKA, DV_HPS, DV_HW0, DV_HA0, DV_HLNW = 0, 16, 24, 49, 66, 70, 74, 78, 82
        PV_NG, PV_BSS, PV_PS, PV_MU, PV_W0, PV_A0, PV_KK, PV_KA, PV_RK, PV_LNW, PV_LNB, PV_HM, PV_SA, PV_SB = \
            0, 8, 24, 28, 45, 49, 53, 57, 61, 65, 69, 73, 74, 82
        carry = sb("carry", [128, 17])
        xt4 = sb("xt4", [128, 4, D])
        crep = xt4
        allst = xt4
        xn = sb("xn", [128, D], BF16)
        ssq = sb("ssq", [128, 8])
        hT = sb("hT", [128, 8, 512], BF16)
        hTh = sb("hTh", [128, 8, 16], BF16)
        ap_ = sb("aprime", [128, 512])
        ucur = sb("ucur", [128, 528])
        uh = sb("uh", [128, 4, 16])
        diffT = sb("diffT", [128, 512], BF16)
        mixp = sb("mixp", [128, 4, 512], BF16)
        twa = sb("twa", [128, 512], BF16)
        r32 = sb("r32", [128, 528])
        k32 = sb("k32", [128, 528])
        v32 = sb("v32", [128, 512])
        z32 = sb("z32", [128, 512])
        wa32 = z32
        pta, ptb = r32, k32
        f = {n: sb(n, [128, 512]) for n in ("lw", "cs", "Ep", "En", "Epp", "Enh", "a32", "kkr", "lnn", "kp", "bb")}
        f["kk"] = f["kkr"]
        f["rn"] = f["lnn"]
        f["g2"] = f["lnn"]
        f["t2"] = f["lw"]
        f["bon"] = f["cs"]
        f["tmpc"] = f["cs"]
        zv, th, sz = f["lw"], f["cs"], f["Ep"]
        sqb = sb("sqb", [128, 512], BF16)
        rkb = sqb
        ctb = sb("ctb", [128, 512], BF16)
        dtb = sb("dtb", [128, 512], BF16)
        arT2 = [sb(f"arT{i}", [128, 8, 128], BF16) for i in range(2)]
        bkT2 = [sb(f"bkT{i}", [128, 8, 128], BF16) for i in range(2)]
        bhT = sb("bhT", [128, 8, 64], BF16)
        khT = sb("khT", [128, 8, 64], BF16)
        vb = sb("vb", [128, 8, 64], BF16)
        Zt = sb("Zt", [128, 8, 128], BF16)
        BKV = sb("BKV", [128, 8, 192], BF16)
        NA = sb("NA", [128, 8, 128], BF16)
        N1 = sb("N1", [128, 8, 64])
        T1f = sb("T1f", [128, 8, 64])
        Rb = sb("Rb", [128, 8, 64], BF16)
        TT1 = sb("TT1", [128, 8, 64], BF16)
        TT1f = sb("TT1f", [128, 8, 64])
        AK = sb("AK", [128, 8, 128], BF16)
        La = sb("La", [128, 8, 64], BF16)
        Lb = sb("Lb", [128, 8, 64], BF16)
        NPa = sb("NPa", [128, 8, 128], BF16)
        NPb = sb("NPb", [128, 8, 128], BF16)
        TT = sb("TT", [128, 8, 64], BF16)
        WU = sb("WU", [128, 8, 128], BF16)
        MT32 = sb("MT32", [128, 8, 64])
        G32 = sb("G32", [128, 8, 64])
        dPC2 = [sb(f"diagPC{i}", [128, 8, 64]) for i in range(2)]
        QhT = sb("QhT", [128, 8, 64], BF16)
        Yloc = sb("Yloc", [128, 8, 64])
        yfull = MT32
        qtb = sb("qtb", [128, 8, 64], BF16)
        stb = sb("stb", [128, 8, 128], BF16)
        st32 = [sb(f"st32_{hp}", [128, 128]) for hp in range(4)]
        McT = sb("McT", [128, 4, 64])
        Sig = sb("Sig", [128, 4, 64])
        Sigb = sb("Sigb", [128, 4, 64], BF16)
        ft1 = sb("ft1", [128, 4, 64])
        mixp2 = mixp
        mixr = hT
        yf2 = yfull
        qt2 = qtb
        ct2, dt2 = ctb, dtb
        ycor = Yloc
        cen = G32
        sq2 = v32
        gnb = QhT
        gst = sb("gst", [128, 32])
        tmpm = f["Ep"]
        xo = [f["lw"], f["cs"]]
        ss2 = sb("ss2", [128, 4])

        psA = [ps("psA0", [128, 512]), ps("psA1", [128, 512])]
        psM = ps("psM", [128, 512])
        psT = ps("psT", [128, 1024], BF16)
        psT2 = ps("psT2", [128, 1024], BF16)
        psW = ps("psW", [128, 8, 128])
        psN = ps("psN", [128, 8, 64])

        es.enter_context(nc.cleanup_on_exit())
        sems = {e: nc.alloc_semaphore(name=e) for e in COMPUTE}
        dsems = [nc.alloc_semaphore(name=f"d{i}") for i in range(NDMASEM)]
        ccsem = nc.alloc_semaphore(name="ccs")

        W = lambda b, k=None: (b, k)

        def pvc(c0, n=1):
            return pv[:, c0:c0 + n]

        def dvc(c0, n=1):
            return dv[:, c0:c0 + n]

        crepv = xt4[:, 0, :].rearrange("p (a b) -> p a b", b=128)
        allv = xt4[:, :, :].rearrange("p a (b n) -> p (a b) n", n=128)
        P.dma(lambda e: e.dma_start(out=cst[:], in_=cst_d), writes=[W(cst)])
        P.dma(lambda e: e.dma_start(out=pv[:], in_=pv_d), writes=[W(pv)])
        P.dma(lambda e: e.dma_start(out=crepv, in_=crep_d), writes=[W(xt4)])
        P.dma(lambda e: e.dma_start(out=bc[:], in_=bc_d), writes=[W(bc)], q="act")
        P.dma(lambda e: e.dma_start(out=stg[0][:, 0:512], in_=wup_d), writes=[W(stg[0])], q="act")
        P.op("dve", lambda e: e.tensor_copy(out=wupb[:], in_=stg[0][:, 0:512]), reads=[W(stg[0])], writes=[W(wupb)])
        P.dma(lambda e: e.dma_start(out=stg[1][:, 0:512].rearrange("p (g d) -> p g d", g=4),
                                    in_=poolw_d.rearrange("g c d -> c g d")), writes=[W(stg[1])], q="act")
        P.op("dve", lambda e: e.tensor_copy(out=poolwb[:].rearrange("p g d -> p (g d)"), in_=stg[1][:, 0:512]),
             reads=[W(stg[1])], writes=[W(poolwb)])
        P.op("pool", lambda e: e.tensor_copy(out=identb[:], in_=cst[:, C_I128:C_I128 + 128]), reads=[W(cst)], writes=[W(identb)])
        P.op("pool", lambda e: e.tensor_copy(out=bob[:], in_=cst[:, C_BO:C_BO + 128]), reads=[W(cst)], writes=[W(bob)])
        for hp in range(4):
            P.op("pool", lambda e, hp=hp: e.memset(st32[hp][:, 0:64], 0.0), writes=[W(st32[hp])])
            P.op("pool", lambda e, hp=hp: e.tensor_copy(out=st32[hp][:, 64:128], in_=cst[:, C_I64:C_I64 + 64]),
                 reads=[W(cst), W(st32[hp])], writes=[W(st32[hp])])
        def dscal(dst, src, n, m, a_, key):
            if a_ is None:
                P.op("dve", lambda e: e.tensor_scalar(out=dvc(dst, n), in0=pvc(src, n), scalar1=m, scalar2=None, op0=ALU.mult),
                     reads=[W(pv)], writes=[W(dv, key)])
            else:
                P.op("dve", lambda e: e.tensor_scalar(out=dvc(dst, n), in0=pvc(src, n), scalar1=m, scalar2=a_, op0=ALU.mult,
                                                       op1=ALU.add), reads=[W(pv)], writes=[W(dv, key)])
        dscal(DV_OMM, PV_MU, 17, -1.0, 1.0, "omm")
        dscal(DV_OMKA, PV_KA, 4, -1.0, 1.0, "omka")
        dscal(DV_HPS, PV_PS, 4, 0.5, None, "hps")
        dscal(DV_HW0, PV_W0, 4, 0.5, None, "hw0")
        dscal(DV_HA0, PV_A0, 4, 0.5, None, "ha0")
        dscal(DV_HLNW, PV_LNW, 4, 0.5, None, "hlnw")

        P.op("dve", lambda e: e.memset(psM[:, 0:32], 0.0), writes=[W(psM)])
        nst_ = [0]

        def next_stg():
            b_ = stg[nst_[0] % 2]
            nst_[0] += 1
            return b_
        for dc in range(8):
            for q in range(6):
                s_ = next_stg()
                P.dma(lambda e, dc=dc, q=q, s_=s_: e.dma_start(out=s_[:, 0:512],
                                                               in_=wada_d[dc * 128:(dc + 1) * 128, q * 512:(q + 1) * 512]),
                      writes=[W(s_)], q=("sp" if q % 2 == 0 else "act"))
                if q < 4:
                    for kk_ in range(4):
                        k = q * 4 + kk_
                        P.op("pe", lambda e, dc=dc, k=k, kk_=kk_, s_=s_: e.matmul(
                            psM[:, k:k + 1], lhsT=s_[:, kk_ * 128:(kk_ + 1) * 128], rhs=crepv[:, dc, 0:1], start=False,
                            stop=False, skip_group_check=True), reads=[W(s_), W(xt4)], writes=[W(psM)])
                else:
                    hf = q - 4
                    P.op("pe", lambda e, dc=dc, hf=hf, s_=s_: e.matmul(psA[hf][:, :], lhsT=crepv[:, dc, :], rhs=s_[:, 0:512],
                                                                     start=(dc == 0), stop=(dc == 7)),
                         reads=[W(s_), W(xt4)], writes=[W(psA[hf])])
        P.op("dve", lambda e: e.tensor_tensor(out=dvc(DV_SS, 16), in0=psM[:, 0:16], in1=pvc(PV_BSS, 16), op=ALU.add),
             reads=[W(psM), W(pv)], writes=[W(dv, "ss")])
        P.op("dve", lambda e: e.scalar_tensor_tensor(out=dvc(DV_GM, 8), in0=dvc(DV_SS + 8, 8), scalar=1.0,
                                                      in1=pvc(PV_NG, 8), op0=ALU.add, op1=ALU.mult),
             reads=[W(dv, "ss"), W(pv)], writes=[W(dv, "gm")])
        for hf in range(2):
            P.op("dve", lambda e, hf=hf: e.tensor_tensor(out=bc[:, 0, hf * 512:(hf + 1) * 512], in0=psA[hf][:, :],
                                                         in1=bc[:, 0, hf * 512:(hf + 1) * 512], op=ALU.add),
                 reads=[W(psA[hf]), W(bc)], writes=[W(bc)])

        P.op("dve", lambda e: e.memset(psM[:, 0:32], 0.0), reads=[W(psM)], writes=[W(psM)])
        WIB_R = []
        for dc in range(8):
            for q in range(5):
                s_ = next_stg()
                P.dma(lambda e, dc=dc, q=q, s_=s_: e.dma_start(out=s_[:, 0:640],
                                                               in_=win_d[dc * 128:(dc + 1) * 128, q * 640:(q + 1) * 640]),
                      writes=[W(s_)], q=("sp" if q % 2 == 0 else "act"))
                for kk_ in range(5):
                    cc = q * 5 + kk_
                    P.op("pe", lambda e, dc=dc, cc=cc, kk_=kk_, s_=s_: e.matmul(
                        psM[:, cc:cc + 1], lhsT=s_[:, kk_ * 128:(kk_ + 1) * 128], rhs=dv[:, DV_SS + dc:DV_SS + dc + 1],
                        start=False, stop=False, skip_group_check=True), reads=[W(s_), W(dv, "ss")], writes=[W(psM)])
                eng = ("dve", "pool", "act")[(dc * 5 + q) % 3]
                if eng == "act":
                    P.op("act", lambda e, dc=dc, q=q, s_=s_: e.activation(out=wib[:, dc, q * 640:(q + 1) * 640], in_=s_[:, 0:640],
                                                                         func=AF.Copy, scale=dvc(DV_GM + dc)),
                         reads=[W(s_), W(dv, "gm")], writes=[W(wib, (q, dc))])
                else:
                    P.op(eng, lambda e, dc=dc, q=q, s_=s_: e.tensor_scalar(out=wib[:, dc, q * 640:(q + 1) * 640], in0=s_[:, 0:640],
                                                                          scalar1=dvc(DV_GM + dc), scalar2=None, op0=ALU.mult),
                         reads=[W(s_), W(dv, "gm")], writes=[W(wib, (q, dc))])
                WIB_R.append(W(wib, (q, dc)))
        P.op("dve", lambda e: e.tensor_copy(out=dvc(DV_BP, 25), in_=psM[:, 0:25]), reads=[W(psM)], writes=[W(dv, "bp")])

        P.checkpoint('setup')
        def rms_rstd(src_cols, ncol, out_ap_cols):
            P.op("act", lambda e: e.activation(out=out_ap_cols, in_=src_cols, func=AF.Ln, bias=1e-6, scale=1.0 / D),
                 reads=[W(ssq)], writes=[W(ssq)])
            P.op("act", lambda e: e.activation(out=out_ap_cols, in_=out_ap_cols, func=AF.Exp, scale=-0.5),
                 reads=[W(ssq)], writes=[W(ssq)])

        P.dma(lambda e: e.dma_start(out=xt4[0:16, 0, :], in_=x_d[0:16, :]), writes=[W(xt4)])
        P.op("act", lambda e: e.activation(out=xn[0:16, :], in_=xt4[0:16, 0, :], func=AF.Square, accum_out=ssq[0:16, 0:1]),
             reads=[W(xt4)], writes=[W(xn), W(ssq)])
        rms_rstd(ssq[0:16, 0:1], 1, ssq[0:16, 4:5])
        P.op("dve", lambda e: e.tensor_scalar(out=xn[0:16, :], in0=xt4[0:16, 0, :], scalar1=ssq[0:16, 4:5], scalar2=None,
                                               op0=ALU.mult), reads=[W(xt4), W(ssq)], writes=[W(xn)])
        for dc in range(8):
            P.op("pe", lambda e, dc=dc: e.transpose(out=psT[:, dc * 16:(dc + 1) * 16], in_=xn[0:16, dc * 128:(dc + 1) * 128],
                                                    identity=identb[0:16, 0:16]),
                 reads=[W(xn), W(identb)], writes=[W(psT)])
        P.op("dve", lambda e: e.tensor_copy(out=hTh[:].rearrange("p a b -> p (a b)"), in_=psT[:, 0:128]),
             reads=[W(psT)], writes=[W(hTh)])
        for g in range(4):
            for dc in range(8):
                P.op("pe", lambda e, g=g, dc=dc: e.matmul(psM[:, g * 16:(g + 1) * 16], lhsT=wib[:, dc, g * 128:(g + 1) * 128],
                                                        rhs=hTh[:, dc, :], start=(dc == 0), stop=(dc == 7)),
                     reads=WIB_R + [W(hTh)], writes=[W(psM)])
        for g in range(4):
            P.op("dve", lambda e, g=g: e.tensor_scalar(out=uh[:, g, :], in0=psM[:, g * 16:(g + 1) * 16],
                                                       scalar1=dvc(DV_BP + g), scalar2=pvc(PV_HM), op0=ALU.add, op1=ALU.mult),
                 reads=[W(psM), W(dv, "bp"), W(pv)], writes=[W(uh, g)])
        for sc in range(17):
            for dc in range(8):
                P.op("pe", lambda e, sc=sc, dc=dc: e.matmul(psW[:, :, :].rearrange("p a b -> p (a b)")[:, sc * 16:(sc + 1) * 16],
                                                          lhsT=wib[:, dc, (8 + sc) * 128:(9 + sc) * 128], rhs=hTh[:, dc, :],
                                                          start=(dc == 0), stop=(dc == 7)),
                     reads=WIB_R + [W(hTh)], writes=[W(psW)])
        psWf = psW[:, :, :].rearrange("p a b -> p (a b)")
        P.op("dve", lambda e: e.tensor_tensor(out=carry[:, :], in0=psWf[:, 15:272:16], in1=dvc(DV_BP + 8, 17), op=ALU.add),
             reads=[W(psW), W(dv, "bp")], writes=[W(carry)])
        P.op("dve", lambda e: e.tensor_scalar(out=carry[:, :], in0=carry[:, :], scalar1=pvc(PV_HM), scalar2=None, op0=ALU.mult),
             reads=[W(carry), W(pv)], writes=[W(carry)])
        P.op("dve", lambda e: e.tensor_tensor(out=carry[:, :], in0=carry[:, :], in1=dvc(DV_BP + 8, 17), op=ALU.subtract),
             reads=[W(carry), W(dv, "bp")], writes=[W(carry)])

        P.checkpoint('halo')
        bank = [0]

        def inproj(cc):
            b = psA[bank[0] % 2]
            bank[0] += 1
            for dc in range(8):
                P.op("pe", lambda e, dc=dc, b=b: e.matmul(b[:, :], lhsT=wib[:, dc, cc * 128:(cc + 1) * 128], rhs=hT[:, dc, :],
                                                        start=(dc == 0), stop=(dc == 7)),
                     reads=WIB_R + [W(hT)], writes=[W(b)])
            return b

        def evac_shift(cc, dst):
            sc = cc - 8
            b = inproj(cc)
            P.op("act", lambda e: e.activation(out=ap_[:, :], in_=b[:, :], func=AF.Identity, bias=dvc(DV_BP + cc),
                                               scale=dvc(DV_OMM + sc)),
                 reads=[W(b), W(dv, "bp"), W(dv, "omm")], writes=[W(ap_)])
            P.op("dve", lambda e: e.scalar_tensor_tensor(out=dst[:, 1:512], in0=b[:, 0:511], scalar=pvc(PV_MU + sc),
                                                          in1=ap_[:, 1:512], op0=ALU.mult, op1=ALU.add),
                 reads=[W(b), W(ap_), W(pv)], writes=[W(dst)])
            P.op("dve", lambda e: e.scalar_tensor_tensor(out=dst[:, 0:1], in0=carry[:, sc:sc + 1], scalar=pvc(PV_MU + sc),
                                                          in1=ap_[:, 0:1], op0=ALU.mult, op1=ALU.add),
                 reads=[W(carry, sc), W(ap_), W(pv)], writes=[W(dst)])
            P.op("act", lambda e: e.activation(out=carry[:, sc:sc + 1], in_=b[:, 511:512], func=AF.Copy),
                 reads=[W(b)], writes=[W(carry, sc)])

        def v3(buf, lo=None, hi=None):
            a = buf[:, 0:512].rearrange("p (c t) -> p c t", t=64)
            return a if lo is None else a[:, :, lo:hi]

        def gen_AB(s):
            r0 = HALO + s * 512
            yield
            P.dma(lambda e, r0=r0: e.dma_start(out=xt4[:, :, :], in_=x_d[r0:r0 + 512, :].rearrange("(a p) d -> p a d", p=128)),
                  writes=[W(xt4)])
            yield
            for tt in range(4):
                P.op("act", lambda e, tt=tt: e.activation(out=xn[:, :], in_=xt4[:, tt, :], func=AF.Square,
                                                          accum_out=ssq[:, tt:tt + 1]),
                     reads=[W(xt4)], writes=[W(xn), W(ssq)])
            rms_rstd(ssq[:, 0:4], 4, ssq[:, 4:8])
            yield
            for tt in range(4):
                P.op("dve", lambda e, tt=tt: e.tensor_scalar(out=xn[:, :], in0=xt4[:, tt, :], scalar1=ssq[:, 4 + tt:5 + tt],
                                                             scalar2=None, op0=ALU.mult),
                     reads=[W(xt4), W(ssq)], writes=[W(xn)])
                yield
                pt = psT if tt % 2 == 0 else psT2
                for dc in range(8):
                    P.op("pe", lambda e, dc=dc, pt=pt: e.transpose(out=pt[:, dc * 128:(dc + 1) * 128],
                                                                   in_=xn[:, dc * 128:(dc + 1) * 128], identity=identb[:, :]),
                         reads=[W(xn), W(identb)], writes=[W(pt)])
                eng = "act" if tt % 2 == 0 else "dve"
                if eng == "act":
                    P.op("act", lambda e, tt=tt, pt=pt: e.activation(out=hT[:, :, tt * 128:(tt + 1) * 128],
                                                                     in_=pt[:, :].rearrange("p (a b) -> p a b", b=128), func=AF.Copy),
                         reads=[W(pt)], writes=[W(hT)])
                else:
                    P.op("dve", lambda e, tt=tt, pt=pt: e.tensor_copy(out=hT[:, :, tt * 128:(tt + 1) * 128],
                                                                      in_=pt[:, :].rearrange("p (a b) -> p a b", b=128)),
                         reads=[W(pt)], writes=[W(hT)])

            P.checkpoint(f'A{s}')
            yield
            for g in range(4):
                win = 2 ** (g + 1)
                u = ucur
                yield
                b = inproj(g)
                P.op("act", lambda e, g=g, b=b: e.activation(out=ucur[:, 16:528], in_=b[:, :], func=AF.Identity,
                                                             bias=dvc(DV_BP + g), scale=1.0),
                     reads=[W(b), W(dv, "bp")], writes=[W(ucur)])
                P.op("pool", lambda e, g=g: e.tensor_copy(out=ucur[:, 0:16], in_=uh[:, g, :]), reads=[W(uh, g), W(ucur)],
                     writes=[W(ucur)])
                yield
                b = inproj(4 + g)
                P.op("act", lambda e, g=g, b=b: e.activation(out=zv[:, :], in_=b[:, :], func=AF.Identity,
                                                             bias=dvc(DV_BP + 4 + g), scale=1.0),
                     reads=[W(b), W(dv, "bp")], writes=[W(zv)])
                P.op("act", lambda e: e.activation(out=th[:, :], in_=zv[:, :], func=AF.Tanh, scale=0.5),
                     reads=[W(zv)], writes=[W(th)])
                P.op("dve", lambda e: e.scalar_tensor_tensor(out=sz[:, :], in0=th[:, :], scalar=1.0, in1=zv[:, :],
                                                               op0=ALU.add, op1=ALU.mult),
                     reads=[W(th), W(zv)], writes=[W(sz)])
                P.op("pool", lambda e, u=u: e.tensor_tensor(out=pta[:, 1:528], in0=u[:, 1:528], in1=u[:, 0:527], op=ALU.add),
                     reads=[W(u)], writes=[W(pta)])
                cur, oth = pta, ptb
                sh = 2
                lo = 1
                while sh < win:
                    lo2 = lo + sh
                    P.op("pool", lambda e, cur=cur, oth=oth, lo2=lo2, sh=sh: e.tensor_tensor(
                        out=oth[:, lo2:528], in0=cur[:, lo2:528], in1=cur[:, lo2 - sh:528 - sh], op=ALU.add),
                        reads=[W(cur)], writes=[W(oth)])
                    cur, oth = oth, cur
                    lo = lo2
                    sh *= 2
                P.op("dve", lambda e, cur=cur, u=u, win=win: e.scalar_tensor_tensor(
                    out=diffT[:, :], in0=cur[:, 16:528], scalar=1.0 / win, in1=u[:, 16:528], op0=ALU.mult, op1=ALU.subtract),
                    reads=[W(cur), W(u)], writes=[W(diffT)])
                if s == 0:
                    P.op("pool", lambda e, cur=cur, oth=oth, g=g: e.tensor_tensor(
                        out=oth[:, 0:16], in0=cur[:, 16:32], in1=cst[:, C_INV + g * 16:C_INV + (g + 1) * 16], op=ALU.mult),
                        reads=[W(cur), W(cst)], writes=[W(oth)])
                    P.op("pool", lambda e, oth=oth, u=u: e.tensor_tensor(out=diffT[:, 0:16], in0=oth[:, 0:16], in1=u[:, 16:32],
                                                                         op=ALU.subtract),
                         reads=[W(oth), W(u), W(diffT)], writes=[W(diffT)])
                P.op("pool", lambda e, u=u, g=g: e.tensor_copy(out=uh[:, g, :], in_=u[:, 512:528]), reads=[W(u)],
                     writes=[W(uh, g)])
                P.op("pe", lambda e, g=g: e.matmul(psM[:, :], lhsT=poolwb[:, g, :], rhs=diffT[:, :], start=True, stop=True),
                     reads=[W(poolwb), W(diffT)], writes=[W(psM)])
                P.op("dve", lambda e, g=g: e.scalar_tensor_tensor(out=mixp[:, g, :], in0=psM[:, :], scalar=dvc(DV_HPS + g),
                                                                  in1=sz[:, :], op0=ALU.mult, op1=ALU.mult),
                     reads=[W(psM), W(sz), W(dv, "hps")], writes=[W(mixp)])
            yield
            P.dma(lambda e, s=s: e.dma_start(out=sp_mixp[s], in_=mixp[:, :, :]), reads=[W(mixp)], writes=[W(dr["mixp"], s)])

            P.checkpoint(f'B{s}')
            yield
            evac_shift(24, wa32)
            P.op("act", lambda e: e.activation(out=twa[0:64, :], in_=wa32[0:64, :], func=AF.Tanh), reads=[W(wa32)],
                 writes=[W(twa, 0)])
            P.op("pool", lambda e: e.tensor_copy(out=twa[64:128, :], in_=wa32[64:128, :]), reads=[W(wa32)], writes=[W(twa, 1)])
            yield

        def gen_prep(s, hp):
            arT, bkT, diagPC = arT2[hp % 2], bkT2[hp % 2], dPC2[hp % 2]
            yield
            evac_shift(8 + hp, r32)
            yield
            evac_shift(12 + hp, k32)
            yield
            evac_shift(16 + hp, v32)
            yield
            evac_shift(20 + hp, z32)
            hs = slice(hp * 128, (hp + 1) * 128)
            yield
            P.op("pe", lambda e, hs=hs: e.matmul(psM[:, :], lhsT=wupb[0:64, hs], rhs=twa[0:64, :], start=True, stop=True),
                 reads=[W(wupb), W(twa, 0)], writes=[W(psM)])
            P.op("act", lambda e, hp=hp: e.activation(out=f["lw"][:, :], in_=psM[:, :], func=AF.Tanh, bias=dvc(DV_HW0 + hp),
                                                      scale=0.5), reads=[W(psM), W(dv, "hw0")], writes=[W(f["lw"])])
            P.op("dve", lambda e: e.tensor_scalar(out=f["lw"][:, :], in0=f["lw"][:, :], scalar1=LOGK, scalar2=LOGK,
                                                   op0=ALU.mult, op1=ALU.add), reads=[W(f["lw"])], writes=[W(f["lw"])])
            P.op("dve", lambda e: e.tensor_tensor_scan(out=f["cs"][:, :], data0=cst[:, C_SC:C_SC + 512], data1=f["lw"][:, :],
                                                        initial=0.0, op0=ALU.mult, op1=ALU.add),
                 reads=[W(f["lw"]), W(cst)], writes=[W(f["cs"])])
            P.op("act", lambda e: e.activation(out=f["Ep"][:, :], in_=f["cs"][:, :], func=AF.Exp), reads=[W(f["cs"])],
                 writes=[W(f["Ep"])])
            P.op("act", lambda e: e.activation(out=f["En"][:, :], in_=f["cs"][:, :], func=AF.Exp, scale=-1.0),
                 reads=[W(f["cs"])], writes=[W(f["En"])])
            P.op("pool", lambda e: e.tensor_copy(out=f["Epp"][:, 1:512], in_=f["Ep"][:, 0:511]), reads=[W(f["Ep"])],
                 writes=[W(f["Epp"])])
            P.op("pool", lambda e: e.memset(v3(f["Epp"], 0, 1), 1.0), reads=[W(f["Epp"])], writes=[W(f["Epp"])])
            P.op("pool", lambda e: e.tensor_tensor(out=v3(f["Enh"]), in0=v3(f["En"]),
                                                   in1=v3(f["Ep"], 63, 64).broadcast_to([128, 8, 64]), op=ALU.mult),
                 reads=[W(f["En"]), W(f["Ep"])], writes=[W(f["Enh"])])
            P.op("pool", lambda e: e.tensor_tensor(out=diagPC[:, :, :],
                                                   in0=cst[:, C_I64:C_I64 + 64].unsqueeze(1).broadcast_to([128, 8, 64]),
                                                   in1=v3(f["Ep"], 63, 64).broadcast_to([128, 8, 64]), op=ALU.mult),
                 reads=[W(cst), W(f["Ep"])], writes=[W(diagPC)])
            yield
            P.op("pe", lambda e, hs=hs: e.matmul(psM[:, :], lhsT=wupb[64:128, hs], rhs=twa[64:128, :], start=True, stop=True),
                 reads=[W(wupb), W(twa, 1)], writes=[W(psM)])
            P.op("act", lambda e, hp=hp: e.activation(out=f["a32"][:, :], in_=psM[:, :], func=AF.Tanh, bias=dvc(DV_HA0 + hp),
                                                      scale=0.5), reads=[W(psM), W(dv, "ha0")], writes=[W(f["a32"])])
            P.op("pool", lambda e: e.tensor_scalar(out=f["a32"][:, :], in0=f["a32"][:, :], scalar1=0.5, scalar2=0.5,
                                                    op0=ALU.mult, op1=ALU.add), reads=[W(f["a32"])], writes=[W(f["a32"])])
            P.op("dve", lambda e, hp=hp: e.tensor_scalar(out=f["kkr"][:, :], in0=k32[:, 0:512], scalar1=pvc(PV_KK + hp),
                                                         scalar2=None, op0=ALU.mult),
                 reads=[W(k32), W(pv)], writes=[W(f["kkr"])])
            P.op("pool", lambda e: e.tensor_tensor(out=sqb[:, :], in0=f["kkr"][:, :], in1=f["kkr"][:, :], op=ALU.mult),
                 reads=[W(f["kkr"])], writes=[W(sqb)])
            yield
            P.op("pe", lambda e: e.matmul(psM[:, :], lhsT=bob[:, :], rhs=sqb[:, :], start=True, stop=True),
                 reads=[W(bob), W(sqb)], writes=[W(psM)])
            P.op("act", lambda e: e.activation(out=f["lnn"][:, :], in_=psM[:, :], func=AF.Ln, bias=1e-24, scale=1.0),
                 reads=[W(psM)], writes=[W(f["lnn"])])
            P.op("act", lambda e: e.activation(out=f["rn"][:, :], in_=f["lnn"][:, :], func=AF.Exp, scale=-0.5),
                 reads=[W(f["lnn"])], writes=[W(f["rn"])])
            P.op("dve", lambda e: e.tensor_tensor(out=f["kk"][:, :], in0=f["kkr"][:, :], in1=f["rn"][:, :], op=ALU.mult),
                 reads=[W(f["kkr"]), W(f["rn"])], writes=[W(f["kk"])])
            P.op("pool", lambda e, hp=hp: e.tensor_scalar(out=f["t2"][:, :], in0=f["a32"][:, :], scalar1=pvc(PV_KA + hp),
                                                          scalar2=dvc(DV_OMKA + hp), op0=ALU.mult, op1=ALU.add),
                 reads=[W(f["a32"]), W(pv), W(dv, "omka")], writes=[W(f["t2"])])
            P.op("dve", lambda e: e.tensor_tensor(out=f["kp"][:, :], in0=k32[:, 0:512], in1=f["t2"][:, :], op=ALU.mult),
                 reads=[W(k32), W(f["t2"])], writes=[W(f["kp"])])
            P.op("pool", lambda e: e.tensor_tensor(out=f["bb"][:, :], in0=f["kk"][:, :], in1=f["a32"][:, :], op=ALU.mult),
                 reads=[W(f["kk"]), W(f["a32"])], writes=[W(f["bb"])])
            P.op("dve", lambda e: e.scalar_tensor_tensor(out=arT[:, :, 0:64], in0=v3(f["kk"]), scalar=-1.0, in1=v3(f["Epp"]),
                                                          op0=ALU.mult, op1=ALU.mult),
                 reads=[W(f["kk"]), W(f["Epp"])], writes=[W(arT, 0)])
            P.op("pool", lambda e: e.tensor_tensor(out=arT[:, :, 64:128], in0=v3(r32), in1=v3(f["Ep"]), op=ALU.mult),
                 reads=[W(r32), W(f["Ep"])], writes=[W(arT, 1)])
            P.op("dve", lambda e: e.tensor_tensor(out=bkT[:, :, 0:64], in0=v3(f["bb"]), in1=v3(f["En"]), op=ALU.mult),
                 reads=[W(f["bb"]), W(f["En"])], writes=[W(bkT, 0)])
            P.op("pool", lambda e: e.tensor_tensor(out=bkT[:, :, 64:128], in0=v3(f["kp"]), in1=v3(f["En"]), op=ALU.mult),
                 reads=[W(f["kp"]), W(f["En"])], writes=[W(bkT, 1)])
            P.op("dve", lambda e: e.tensor_tensor(out=bhT[:, :, :], in0=v3(f["bb"]), in1=v3(f["Enh"]), op=ALU.mult),
                 reads=[W(f["bb"]), W(f["Enh"])], writes=[W(bhT)])
            P.op("pool", lambda e: e.tensor_tensor(out=khT[:, :, :], in0=v3(f["kp"]), in1=v3(f["Enh"]), op=ALU.mult),
                 reads=[W(f["kp"]), W(f["Enh"])], writes=[W(khT)])
            P.op("act", lambda e: e.activation(out=vb[:, :, :], in_=v3(v32), func=AF.Copy), reads=[W(v32)], writes=[W(vb)])
            P.op("dve", lambda e, hp=hp: e.scalar_tensor_tensor(out=rkb[:, :], in0=r32[:, 0:512], scalar=pvc(PV_RK + hp),
                                                                in1=f["kp"][:, :], op0=ALU.mult, op1=ALU.mult),
                 reads=[W(r32), W(f["kp"]), W(pv)], writes=[W(rkb)])
            yield
            P.op("pe", lambda e: e.matmul(psM[:, :], lhsT=bob[:, :], rhs=rkb[:, :], start=True, stop=True),
                 reads=[W(bob), W(rkb)], writes=[W(psM)])
            P.op("act", lambda e: e.activation(out=f["g2"][:, :], in_=z32[:, :], func=AF.Tanh, scale=0.5),
                 reads=[W(z32)], writes=[W(f["g2"])])
            P.op("dve", lambda e: e.scalar_tensor_tensor(out=f["g2"][:, :], in0=f["g2"][:, :], scalar=1.0, in1=z32[:, :],
                                                           op0=ALU.add, op1=ALU.mult),
                 reads=[W(f["g2"]), W(z32)], writes=[W(f["g2"])])
            P.op("dve", lambda e: e.tensor_tensor(out=f["bon"][:, :], in0=psM[:, :], in1=v32[:, :], op=ALU.mult),
                 reads=[W(psM), W(v32)], writes=[W(f["bon"])])
            P.op("dve", lambda e, hp=hp: e.scalar_tensor_tensor(out=f["tmpc"][:, :], in0=f["bon"][:, :],
                                                                scalar=pvc(PV_LNB + hp), in1=f["g2"][:, :],
                                                                op0=ALU.add, op1=ALU.mult),
                 reads=[W(f["bon"]), W(f["g2"]), W(pv)], writes=[W(f["tmpc"])])
            P.op("act", lambda e: e.activation(out=ctb[:, :], in_=f["tmpc"][:, :], func=AF.Copy, scale=0.5),
                 reads=[W(f["tmpc"])], writes=[W(ctb)])
            P.op("pool", lambda e, hp=hp: e.tensor_scalar(out=dtb[:, :], in0=f["g2"][:, :], scalar1=dvc(DV_HLNW + hp),
                                                          scalar2=None, op0=ALU.mult),
                 reads=[W(f["g2"]), W(dv, "hlnw")], writes=[W(dtb)])
            yield
            P.dma(lambda e, s=s, hp=hp: e.dma_start(out=sp_ct[s, hp], in_=ctb[:, :]), reads=[W(ctb)],
                  writes=[W(dr["ct"], (s, hp))])
            yield
            P.dma(lambda e, s=s, hp=hp: e.dma_start(out=sp_dt[s, hp], in_=dtb[:, :]), reads=[W(dtb)],
                  writes=[W(dr["dt"], (s, hp))])

            yield

        def gen_scan(s, hp):
            arT, bkT, diagPC = arT2[hp % 2], bkT2[hp % 2], dPC2[hp % 2]
            P.checkpoint(f'C{s}{hp}')
            units = [(c, h) for c in range(8) for h in range(2)]

            def hsl(h):
                return slice(64 * h, 64 * h + 64)

            yield
            for (src, lo, dstp, off) in ((arT, 0, psT, 0), (bhT, 0, psT, 512), (khT, 0, psT2, 0), (vb, 0, psT2, 512)):
                for c, h in units:
                    P.op("pe", lambda e, src=src, lo=lo, dstp=dstp, off=off, c=c, h=h: e.transpose(
                        out=dstp[hsl(h), off + c * 64:off + (c + 1) * 64], in_=src[hsl(h), c, lo:lo + 64],
                        identity=identb[hsl(h), hsl(h)]),
                        reads=[W(src, 0) if src is arT else W(src), W(identb)], writes=[W(dstp)])
            P.op("act", lambda e: e.activation(out=Zt[:, :, 0:64], in_=psT[:, 0:512].rearrange("p (c t) -> p c t", t=64),
                                               func=AF.Copy), reads=[W(psT)], writes=[W(Zt, 0)])
            P.op("dve", lambda e: e.tensor_copy(out=BKV[:, :, 0:64], in_=psT[:, 512:1024].rearrange("p (c t) -> p c t", t=64)),
                 reads=[W(psT)], writes=[W(BKV, 0)])
            P.op("act", lambda e: e.activation(out=BKV[:, :, 64:128], in_=psT2[:, 0:512].rearrange("p (c t) -> p c t", t=64),
                                               func=AF.Copy), reads=[W(psT2)], writes=[W(BKV, 1)])
            P.op("dve", lambda e: e.tensor_copy(out=BKV[:, :, 128:192],
                                                in_=psT2[:, 512:1024].rearrange("p (c t) -> p c t", t=64)),
                 reads=[W(psT2)], writes=[W(BKV, 2)])

            def mmu(dst, dlo, dhi, lh, llo, lhi, rh, rlo, rhi, rd=(), start=True, stop=True):
                for c, h in units:
                    P.op("pe", lambda e, c=c, h=h: e.matmul(dst[hsl(h), c, dlo:dhi], lhsT=lh[hsl(h), c, llo:lhi],
                                                          rhs=rh[hsl(h), c, rlo:rhi], start=start, stop=stop),
                         reads=list(rd), writes=[W(dst)])

            m2 = cst[:, C_M2:C_M2 + 128].unsqueeze(1).broadcast_to([128, 8, 128])
            mL = cst[:, C_ML:C_ML + 64].unsqueeze(1).broadcast_to([128, 8, 64])
            mS = cst[:, C_M2:C_M2 + 64].unsqueeze(1).broadcast_to([128, 8, 64])
            idb = cst[:, C_I64:C_I64 + 64].unsqueeze(1).broadcast_to([128, 8, 64])
            yield
            mmu(psW, 0, 128, bkT, 0, 64, arT, 0, 128, rd=[W(bkT, 0), W(arT, 0), W(arT, 1)])
            P.op("dve", lambda e: e.tensor_tensor(out=NA[:, :, :], in0=psW[:, :, :], in1=m2, op=ALU.mult),
                 reads=[W(psW), W(cst)], writes=[W(NA)])
            P.op("dve", lambda e: e.tensor_tensor(out=N1[:, :, :], in0=psW[:, :, 0:64], in1=mS, op=ALU.mult),
                 reads=[W(psW), W(cst)], writes=[W(N1)])
            yield
            mmu(psW, 0, 128, bkT, 64, 128, arT, 0, 128, rd=[W(bkT, 1), W(arT, 0), W(arT, 1)])
            P.op("dve", lambda e: e.tensor_tensor(out=AK[:, :, :], in0=psW[:, :, :], in1=m2, op=ALU.mult),
                 reads=[W(psW), W(cst)], writes=[W(AK)])
            yield
            mmu(psN, 0, 64, arT, 0, 64, bkT, 0, 64, rd=[W(bkT, 0), W(arT, 0)])
            P.op("dve", lambda e: e.tensor_tensor(out=La[:, :, :], in0=psN[:, :, :], in1=mL, op=ALU.mult),
                 reads=[W(psN), W(cst)], writes=[W(La)])
            P.op("pool", lambda e: e.tensor_tensor(out=NPa[:, :, 64:128], in0=NA[:, :, 0:64], in1=idb, op=ALU.add),
                 reads=[W(NA), W(cst)], writes=[W(NPa, 1)])
            yield
            mmu(psW, 0, 64, La, 0, 64, NA, 0, 64, rd=[W(La), W(NA)])
            yield
            mmu(psN, 0, 64, NA, 0, 64, La, 0, 64, rd=[W(La), W(NA)])
            P.op("act", lambda e: e.activation(out=NPa[:, :, 0:64], in_=psW[:, :, 0:64], func=AF.Copy),
                 reads=[W(psW)], writes=[W(NPa, 0)])
            P.op("dve", lambda e: e.tensor_copy(out=Lb[:, :, :], in_=psN[:, :, :]), reads=[W(psN)], writes=[W(Lb)])
            Lc, Lo, NPc, NPo = Lb, La, NPa, NPb
            for step in (1, 2, 3):
                mmu(psW, 0, 128, Lc, 0, 64, NPc, 0, 128, rd=[W(Lc), W(NPc, 0), W(NPc, 1)])
                mmu(psN, 0, 64, NPc, 0, 64, Lc, 0, 64, rd=[W(Lc), W(NPc, 0)])
                P.op("act", lambda e, NPo=NPo: e.activation(out=NPo[:, :, 0:64], in_=psW[:, :, 0:64], func=AF.Copy),
                     reads=[W(psW)], writes=[W(NPo, 0)])
                P.op("dve", lambda e, NPo=NPo, NPc=NPc: e.tensor_tensor(out=NPo[:, :, 64:128], in0=psW[:, :, 64:128],
                                                                        in1=NPc[:, :, 64:128], op=ALU.add),
                     reads=[W(psW), W(NPc, 1)], writes=[W(NPo, 1)])
                P.op("act", lambda e, Lo=Lo: e.activation(out=Lo[:, :, :], in_=psN[:, :, :], func=AF.Copy),
                     reads=[W(psN)], writes=[W(Lo)])
                Lc, Lo, NPc, NPo = Lo, Lc, NPo, NPc
            yield
            mmu(psW, 64, 128, Lc, 0, 64, NPc, 64, 128, rd=[W(Lc), W(NPc, 1)])
            yield
            mmu(psN, 0, 64, NPc, 0, 64, Lc, 0, 64, rd=[W(Lc), W(NPc, 0)])
            P.op("dve", lambda e, NPo=NPo, NPc=NPc: e.tensor_tensor(out=NPo[:, :, 64:128], in0=psW[:, :, 64:128],
                                                                    in1=NPc[:, :, 64:128], op=ALU.add),
                 reads=[W(psW), W(NPc, 1)], writes=[W(NPo, 1)])
            P.op("act", lambda e, Lo=Lo: e.activation(out=Lo[:, :, :], in_=psN[:, :, :], func=AF.Copy),
                 reads=[W(psN)], writes=[W(Lo)])
            Lc, Lo, NPc, NPo = Lo, Lc, NPo, NPc
            yield
            mmu(psW, 64, 128, Lc, 0, 64, NPc, 64, 128, rd=[W(Lc), W(NPc, 1)])
            P.op("dve", lambda e, NPc=NPc: e.tensor_tensor(out=TT1[:, :, :], in0=psW[:, :, 64:128], in1=NPc[:, :, 64:128],
                                                           op=ALU.add), reads=[W(psW), W(NPc, 1)], writes=[W(TT1)])
            P.op("pool", lambda e: e.tensor_copy(out=TT1f[:, :, :], in_=TT1[:, :, :]), reads=[W(TT1)], writes=[W(TT1f)])
            yield
            for c, h in units:
                P.op("pe", lambda e, c=c, h=h: e.transpose(out=psT[hsl(h), c * 64:(c + 1) * 64], in_=TT1[hsl(h), c, :],
                                                           identity=identb[hsl(h), hsl(h)]),
                     reads=[W(TT1), W(identb)], writes=[W(psT)])
            P.op("act", lambda e: e.activation(out=T1f[:, :, :], in_=psT[:, 0:512].rearrange("p (c t) -> p c t", t=64),
                                               func=AF.Copy), reads=[W(psT)], writes=[W(T1f)])
            yield
            mmu(psN, 0, 64, N1, 0, 64, T1f, 0, 64, rd=[W(N1), W(T1f)])
            P.op("pool", lambda e: e.tensor_tensor(out=T1f[:, :, :], in0=idb, in1=T1f[:, :, :], op=ALU.subtract),
                 reads=[W(T1f), W(cst)], writes=[W(T1f)])
            P.op("dve", lambda e: e.tensor_tensor(out=Rb[:, :, :], in0=psN[:, :, :], in1=T1f[:, :, :], op=ALU.add),
                 reads=[W(psN), W(T1f)], writes=[W(Rb)])
            yield
            mmu(psN, 0, 64, Rb, 0, 64, TT1, 0, 64, rd=[W(Rb), W(TT1)])
            P.op("dve", lambda e: e.tensor_tensor(out=TT[:, :, :], in0=psN[:, :, :], in1=TT1f[:, :, :], op=ALU.add),
                 reads=[W(psN), W(TT1f)], writes=[W(TT)])
            yield
            mmu(psN, 0, 64, AK, 0, 64, BKV, 128, 192, rd=[W(AK), W(BKV, 2)])
            P.op("act", lambda e: e.activation(out=Zt[:, :, 64:128], in_=psN[:, :, :], func=AF.Copy),
                 reads=[W(psN)], writes=[W(Zt, 1)])
            yield
            mmu(psW, 0, 128, TT, 0, 64, Zt, 0, 128, rd=[W(TT), W(Zt, 0), W(Zt, 1)])
            P.op("act", lambda e: e.activation(out=WU[:, :, :], in_=psW[:, :, :], func=AF.Copy), reads=[W(psW)],
                 writes=[W(WU)])
            yield
            mmu(psN, 0, 64, WU, 0, 64, BKV, 0, 64, rd=[W(WU), W(BKV, 0)])
            P.op("dve", lambda e: e.tensor_tensor(out=MT32[:, :, :], in0=psN[:, :, :], in1=diagPC[:, :, :], op=ALU.add),
                 reads=[W(psN), W(diagPC)], writes=[W(MT32)])
            yield
            for c, h in units:
                P.op("pe", lambda e, c=c, h=h: e.matmul(psN[hsl(h), c, :], lhsT=BKV[hsl(h), c, 0:64],
                                                      rhs=WU[hsl(h), c, 64:128], start=True, stop=False),
                     reads=[W(WU), W(BKV, 0)], writes=[W(psN)])
                P.op("pe", lambda e, c=c, h=h: e.matmul(psN[hsl(h), c, :], lhsT=BKV[hsl(h), c, 64:128],
                                                      rhs=BKV[hsl(h), c, 128:192], start=False, stop=True),
                     reads=[W(BKV, 1), W(BKV, 2)], writes=[W(psN)])
            P.op("act", lambda e: e.activation(out=G32[:, :, :], in_=psN[:, :, :], func=AF.Copy),
                 reads=[W(psN)], writes=[W(G32)])
            yield
            mmu(psN, 0, 64, WU, 0, 64, NA, 64, 128, rd=[W(WU), W(NA)])
            P.op("dve", lambda e: e.tensor_tensor(out=QhT[:, :, :], in0=psN[:, :, :], in1=arT[:, :, 64:128], op=ALU.add),
                 reads=[W(psN), W(arT, 1)], writes=[W(QhT)])
            yield
            for c, h in units:
                P.op("pe", lambda e, c=c, h=h: e.matmul(psN[hsl(h), c, :], lhsT=NA[hsl(h), c, 64:128],
                                                      rhs=WU[hsl(h), c, 64:128], start=True, stop=False),
                     reads=[W(WU), W(NA)], writes=[W(psN)])
                P.op("pe", lambda e, c=c, h=h: e.matmul(psN[hsl(h), c, :], lhsT=AK[hsl(h), c, 64:128],
                                                      rhs=BKV[hsl(h), c, 128:192], start=False, stop=True),
                     reads=[W(AK), W(BKV, 2)], writes=[W(psN)])
            P.op("act", lambda e: e.activation(out=Yloc[:, :, :], in_=psN[:, :, :], func=AF.Copy),
                 reads=[W(psN)], writes=[W(Yloc)])
            stt_ = st32[hp]
            yield
            for c in range(8):
                yield
                P.op("pool", lambda e, c=c, stt_=stt_: e.tensor_copy(out=stb[:, c, :], in_=stt_[:, :]),
                     reads=[W(stt_)], writes=[W(stb, c)])
                for h in range(2):
                    P.op("pe", lambda e, c=c, h=h, stt_=stt_: e.matmul(psM[hsl(h), 0:128], lhsT=MT32[hsl(h), c, :],
                                                                     rhs=stt_[hsl(h), :], start=True, stop=True),
                         reads=[W(MT32), W(stt_)], writes=[W(psM)])
                P.op("dve", lambda e, c=c, stt_=stt_: e.tensor_tensor(out=stt_[:, 0:64], in0=psM[:, 0:64], in1=G32[:, c, :],
                                                                      op=ALU.add),
                     reads=[W(psM), W(G32)], writes=[W(stt_)])
                P.op("act", lambda e, c=c, stt_=stt_: e.activation(out=stt_[:, 64:128], in_=psM[:, 64:128], func=AF.Copy),
                     reads=[W(psM), W(stt_)], writes=[W(stt_)])
            yield
            mmu(psN, 0, 64, QhT, 0, 64, stb, 0, 64, rd=[W(QhT)] + [W(stb, c) for c in range(8)])
            P.op("dve", lambda e: e.tensor_tensor(out=yfull[:, :, :], in0=psN[:, :, :], in1=Yloc[:, :, :], op=ALU.add),
                 reads=[W(psN), W(Yloc)], writes=[W(yfull)])
            yield
            mmu(psN, 0, 64, stb, 64, 128, QhT, 0, 64, rd=[W(QhT)] + [W(stb, c) for c in range(8)])
            P.op("act", lambda e: e.activation(out=qtb[:, :, :], in_=psN[:, :, :], func=AF.Copy), reads=[W(psN)],
                 writes=[W(qtb)])
            yield
            P.dma(lambda e, s=s, hp=hp: e.dma_start(out=sp_y[s, hp], in_=yfull[:, :, :].rearrange("p c t -> p (c t)")),
                  reads=[W(yfull)], writes=[W(dr["y"], (s, hp))])
            yield
            P.dma(lambda e, s=s, hp=hp: e.dma_start(out=sp_q[s, hp], in_=qtb[:, :, :].rearrange("p c t -> p (c t)")),
                  reads=[W(qtb)], writes=[W(dr["q"], (s, hp))])

            yield

        def drain(g):
            for _ in g:
                pass

        def chain_gens(gs):
            for g in gs:
                yield from g

        def interleave(ga, gb, ra=1, rb=1):
            da = db = False
            while not (da and db):
                for _ in range(ra):
                    if not da:
                        try:
                            next(ga)
                        except StopIteration:
                            da = True
                for _ in range(rb):
                    if not db:
                        try:
                            next(gb)
                        except StopIteration:
                            db = True

        drain(gen_AB(0))
        drain(gen_prep(0, 0))
        for s in range(NST):
            for hp in range(4):
                if hp < 3:
                    nxt = [gen_prep(s, hp + 1)]
                elif s + 1 < NST:
                    nxt = [gen_AB(s + 1), gen_prep(s + 1, 0)]
                else:
                    nxt = []
                interleave(gen_scan(s, hp), chain_gens(nxt))

        P.checkpoint('p1')
        for hp in range(4):
            P.dma(lambda e, hp=hp: e.dma_start(out=gin_d[hp * 128:(hp + 1) * 128, :], in_=st32[hp][:, :]),
                  reads=[W(st32[hp])], writes=[W(dr["gin"], hp)], q="pool")
        P.cc(lambda e: e.collective_compute("AllGather", ALU.bypass, replica_groups=[list(range(8))],
                                            ins=[gin_d.opt()], outs=[gout_d.opt()]),
             reads=[W(dr["gin"])], writes=[W(dr["gout"])])
        P.dma(lambda e: e.dma_start(out=allv, in_=gout_d.rearrange("(rh p) n -> p rh n", p=128)),
              reads=[W(dr["gout"])], writes=[W(xt4)], q="pool")
        for kc in range(8):
            for q in range(2):
                s_ = next_stg()
                P.dma(lambda e, kc=kc, q=q, s_=s_: e.dma_start(out=s_[:, 0:512],
                                                               in_=wout_d[kc * 128:(kc + 1) * 128, q * 512:(q + 1) * 512]),
                      writes=[W(s_)], q=("sp" if q % 2 == 0 else "act"))
                P.op("pool", lambda e, kc=kc, q=q, s_=s_: e.tensor_copy(out=wib[:, kc, q * 512:(q + 1) * 512], in_=s_[:, 0:512]),
                     reads=[W(s_)], writes=[W(wib)])
        P.op("dve", lambda e: e.memset(Sig[:, :, :], 0.0), writes=[W(Sig)])
        hsl2 = lambda h: slice(64 * h, 64 * h + 64)
        for r in range(8):
            for hp in range(4):
                for h in range(2):
                    P.op("pe", lambda e, r=r, hp=hp, h=h: e.matmul(psN[hsl2(h), hp, :], lhsT=allv[hsl2(h), r * 4 + hp, 64:128],
                                                                 rhs=cst[hsl2(h), C_I64:C_I64 + 64], start=True, stop=True),
                         reads=[W(xt4), W(cst)], writes=[W(psN)])
            P.op("act", lambda e: e.activation(out=McT[:, :, :], in_=psN[:, 0:4, :], func=AF.Copy), reads=[W(psN)],
                 writes=[W(McT)])
            for hp in range(4):
                for h in range(2):
                    P.op("pe", lambda e, hp=hp, h=h: e.matmul(psM[hsl2(h), hp * 64:(hp + 1) * 64], lhsT=McT[hsl2(h), hp, :],
                                                            rhs=Sig[hsl2(h), hp, :], start=True, stop=True),
                         reads=[W(McT), W(Sig)], writes=[W(psM)])
            P.op("dve", lambda e, r=r: e.tensor_tensor(out=ft1[:, :, :], in0=psM[:, 0:256].rearrange("p (a b) -> p a b", b=64),
                                                       in1=allv[:, r * 4:r * 4 + 4, 0:64], op=ALU.add),
                 reads=[W(psM), W(xt4)], writes=[W(ft1)])
            P.op("dve", lambda e, r=r: e.tensor_scalar(out=ft1[:, :, :], in0=ft1[:, :, :], scalar1=pvc(PV_SA + r), scalar2=None,
                                                       op0=ALU.mult), reads=[W(ft1), W(pv)], writes=[W(ft1)])
            P.op("dve", lambda e, r=r: e.scalar_tensor_tensor(out=Sig[:, :, :], in0=Sig[:, :, :], scalar=pvc(PV_SB + r),
                                                              in1=ft1[:, :, :], op0=ALU.mult, op1=ALU.add),
                 reads=[W(Sig), W(ft1), W(pv)], writes=[W(Sig)])
        P.op("dve", lambda e: e.tensor_copy(out=Sigb[:, :, :], in_=Sig[:, :, :]), reads=[W(Sig)], writes=[W(Sigb)])

        P.checkpoint('xchg')
        out_toks = []
        gst2 = [gst, sb("gst_b", [128, 32])]
        mixp2s = [mixp, sb("mixp2b", [128, 4, 512], BF16)]
        YF = [MT32, N1]
        YC = [Yloc, T1f]
        CE = [G32, TT1f]
        SQ = [dPC2[0], dPC2[1]]
        QT = [qtb, bhT]
        GN = [QhT, khT]
        CT = [ctb, sqb]
        DT = [dtb, diffT]
        TM = [f["Ep"], f["En"]]

        def p2_s0(k):
            s, hp = divmod(k, 4)
            i = k % 2
            if hp == 0:
                mp = mixp2s[s % 2]
                P.dma(lambda e, s=s, mp=mp: e.dma_start(out=mp[:, :, :], in_=sp_mixp[s]), reads=[W(dr["mixp"], s)],
                      writes=[W(mp)])
            P.dma(lambda e, s=s, hp=hp, i=i: e.dma_start(out=YF[i][:, :, :].rearrange("p c t -> p (c t)"), in_=sp_y[s, hp]),
                  reads=[W(dr["y"], (s, hp))], writes=[W(YF[i])])
            P.dma(lambda e, s=s, hp=hp, i=i: e.dma_start(out=QT[i][:, :, :].rearrange("p c t -> p (c t)"), in_=sp_q[s, hp]),
                  reads=[W(dr["q"], (s, hp))], writes=[W(QT[i])], q="act")

        def p2_s1(k):
            s, hp = divmod(k, 4)
            i = k % 2
            for c in range(8):
                for h in range(2):
                    P.op("pe", lambda e, c=c, h=h, hp=hp, i=i: e.matmul(psN[hsl2(h), c, :], lhsT=QT[i][hsl2(h), c, :],
                                                                      rhs=Sigb[hsl2(h), hp, :], start=True, stop=True),
                         reads=[W(QT[i]), W(Sigb)], writes=[W(psN)])
            P.op("dve", lambda e, i=i: e.tensor_tensor(out=YC[i][:, :, :], in0=psN[:, :, :], in1=YF[i][:, :, :], op=ALU.add),
                 reads=[W(psN), W(YF[i])], writes=[W(YC[i])])

        def p2_s2(k):
            s, hp = divmod(k, 4)
            i = k % 2
            g_ = gst2[i]
            P.dma(lambda e, s=s, hp=hp, i=i: e.dma_start(out=CT[i][:, :], in_=sp_ct[s, hp]), reads=[W(dr["ct"], (s, hp))],
                  writes=[W(CT[i])])
            P.dma(lambda e, s=s, hp=hp, i=i: e.dma_start(out=DT[i][:, :], in_=sp_dt[s, hp]), reads=[W(dr["dt"], (s, hp))],
                  writes=[W(DT[i])], q="act")
            P.op("dve", lambda e, i=i, g_=g_: e.tensor_reduce(out=g_[:, 0:8], in_=YC[i][:, :, :], axis=AX.X, op=ALU.add),
                 reads=[W(YC[i])], writes=[W(g_, 0)])
            P.op("dve", lambda e, g_=g_: e.tensor_scalar(out=g_[:, 8:16], in0=g_[:, 0:8], scalar1=1.0 / 64, scalar2=None,
                                                         op0=ALU.mult), reads=[W(g_, 0)], writes=[W(g_, 1)])
            P.op("pool", lambda e, i=i, g_=g_: e.tensor_tensor(out=CE[i][:, :, :], in0=YC[i][:, :, :],
                                                               in1=g_[:, 8:16].unsqueeze(2).broadcast_to([128, 8, 64]),
                                                               op=ALU.subtract),
                 reads=[W(YC[i]), W(g_, 1)], writes=[W(CE[i])])
            P.op("pool", lambda e, i=i: e.tensor_tensor(out=SQ[i][:, :, :], in0=CE[i][:, :, :], in1=CE[i][:, :, :], op=ALU.mult),
                 reads=[W(CE[i])], writes=[W(SQ[i])])
            P.op("dve", lambda e, i=i, g_=g_: e.tensor_reduce(out=g_[:, 16:24], in_=SQ[i][:, :, :], axis=AX.X, op=ALU.add),
                 reads=[W(SQ[i])], writes=[W(g_, 2)])
            P.op("act", lambda e, g_=g_: e.activation(out=g_[:, 24:32], in_=g_[:, 16:24], func=AF.Ln, bias=64e-5, scale=1.0 / 64),
                 reads=[W(g_, 2)], writes=[W(g_, 3)])
            P.op("act", lambda e, g_=g_: e.activation(out=g_[:, 24:32], in_=g_[:, 24:32], func=AF.Exp, scale=-0.5),
                 reads=[W(g_, 3)], writes=[W(g_, 3)])
            P.op("dve", lambda e, i=i, g_=g_: e.tensor_tensor(out=GN[i][:, :, :], in0=CE[i][:, :, :],
                                                              in1=g_[:, 24:32].unsqueeze(2).broadcast_to([128, 8, 64]),
                                                              op=ALU.mult),
                 reads=[W(CE[i]), W(g_, 3)], writes=[W(GN[i])])

        def p2_s3(k):
            s, hp = divmod(k, 4)
            i = k % 2
            slot = (s % 2) * 4 + hp
            pt = psT if i == 0 else psT2
            for c in range(8):
                for h in range(2):
                    P.op("pe", lambda e, c=c, h=h, i=i, pt=pt: e.transpose(out=pt[hsl2(h), c * 64:(c + 1) * 64],
                                                                           in_=GN[i][hsl2(h), c, :],
                                                                           identity=identb[hsl2(h), hsl2(h)]),
                         reads=[W(GN[i]), W(identb)], writes=[W(pt)])
            P.op("dve", lambda e, i=i, pt=pt: e.tensor_tensor(out=TM[i][:, :], in0=pt[:, 0:512], in1=DT[i][:, :], op=ALU.mult),
                 reads=[W(pt), W(DT[i])], writes=[W(TM[i])])
            P.op("pool", lambda e, i=i, slot=slot: e.tensor_tensor(out=hT[:, slot, :], in0=TM[i][:, :], in1=CT[i][:, :], op=ALU.add),
                 reads=[W(TM[i]), W(CT[i])], writes=[W(hT, slot)])

        def p2_out(s):
            mp = mixp2s[s % 2]
            r0 = HALO + s * 512
            P.dma(lambda e, r0=r0: e.dma_start(out=xt4[:, :, :], in_=x_d[r0:r0 + 512, :].rearrange("(a p) d -> p a d", p=128)),
                  writes=[W(xt4)], q="act")
            for tt in range(4):
                ts_ = slice(tt * 128, (tt + 1) * 128)
                for hf in range(2):
                    for kc in range(8):
                        if kc < 4:
                            lh = mp[:, kc, ts_]
                        else:
                            lh = hT[:, (s % 2) * 4 + kc - 4, ts_]
                        P.op("pe", lambda e, kc=kc, hf=hf, lh=lh: e.matmul(
                            psA[hf][:, :], lhsT=lh, rhs=wib[:, kc, hf * 512:(hf + 1) * 512],
                            start=(kc == 0), stop=(kc == 7)),
                            reads=[W(mp), W(wib)] + [W(hT, (s % 2) * 4 + i_) for i_ in range(4)], writes=[W(psA[hf])])
                    P.op("dve", lambda e, hf=hf: e.tensor_tensor(out=xo[hf][:, :], in0=psA[hf][:, :],
                                                                 in1=bc[:, 0, hf * 512:(hf + 1) * 512], op=ALU.mult),
                         reads=[W(psA[hf]), W(bc)], writes=[W(xo[hf])])
                    P.op("pool", lambda e, tt=tt, hf=hf: e.tensor_tensor(out=xt4[:, tt, hf * 512:(hf + 1) * 512], in0=xo[hf][:, :],
                                                                         in1=xt4[:, tt, hf * 512:(hf + 1) * 512], op=ALU.add),
                         reads=[W(xo[hf]), W(xt4)], writes=[W(xt4)])
                P.op("act", lambda e, tt=tt: e.activation(out=xn[:, :], in_=xt4[:, tt, :], func=AF.Square, accum_out=ss2[:, 0:1]),
                     reads=[W(xt4)], writes=[W(xn), W(ss2)])
                P.op("act", lambda e: e.activation(out=ss2[:, 1:2], in_=ss2[:, 0:1], func=AF.Ln, bias=1e-6, scale=1.0 / D),
                     reads=[W(ss2)], writes=[W(ss2)])
                P.op("act", lambda e: e.activation(out=ss2[:, 1:2], in_=ss2[:, 1:2], func=AF.Exp, scale=-0.5),
                     reads=[W(ss2)], writes=[W(ss2)])
                P.op("dve", lambda e, tt=tt: e.scalar_tensor_tensor(out=xt4[:, tt, :], in0=xt4[:, tt, :], scalar=ss2[:, 1:2],
                                                                    in1=bc[:, 1, :], op0=ALU.mult, op1=ALU.mult),
                     reads=[W(xt4), W(ss2), W(bc)], writes=[W(xt4)])
                ro = s * 512 + tt * 128
                out_toks.append(P.dma(lambda e, ro=ro, tt=tt: e.dma_start(out=out_d[ro:ro + 128, :], in_=xt4[:, tt, :]),
                                      reads=[W(xt4)]))

        NK = NST * 4
        for t in range(NK + 4):
            if t - 3 >= 0 and t - 3 < NK:
                p2_s3(t - 3)
                if (t - 3) % 4 == 3:
                    p2_out((t - 3) // 4)
            if 0 <= t - 2 < NK:
                p2_s2(t - 2)
            if 0 <= t - 1 < NK:
                p2_s1(t - 1)
            if t < NK:
                p2_s0(t)
        fin = {}
        for sk, v in [t_ for t_ in out_toks if t_ is not None]:
            fin[sk] = max(fin.get(sk, 0), v)
        P.ops["sp"].append((list(fin.items()), None, None))

        needed = {e: set() for e in COMPUTE}
        for name in P.ops:
            for wl, f_, tok in P.ops[name]:
                for sk, v in wl:
                    if isinstance(sk, str):
                        needed[sk].add(v)
        rank = {}
        for e in COMPUTE:
            for i, v in enumerate(sorted(needed[e])):
                rank[(e, v)] = i + 1

        def semof(sk):
            if isinstance(sk, str):
                return sems[sk]
            return dsems[sk[1]] if sk[0] == "d" else ccsem

        def mk(name):
            def fn(e):
                for wl, f_, tok in P.ops[name]:
                    for sk, v in wl:
                        e.wait_ge(semof(sk), rank[(sk, v)] if isinstance(sk, str) else v)
                    if f_ is None:
                        continue
                    ins = f_(e)
                    sk, v = tok
                    if isinstance(sk, str):
                        if (sk, v) in rank:
                            ins.then_inc(sems[sk], 1)
                    elif sk[0] == "d":
                        ins.then_inc(dsems[sk[1]], 16)
                    else:
                        ins.then_inc(ccsem)
            return fn

        with nc.Block() as block:
            block.sync(mk("sp"))
            block.tensor(mk("pe"))
            block.scalar(mk("act"))
            block.vector(mk("dve"))
            block.gpsimd(mk("pool"))
    return nc


_NC_CACHE = {}


def _consts():
    c = np.zeros((128, CW), np.float32)
    p = np.arange(128)
    s = (p % 64)[:, None]
    t = np.arange(64)[None, :]
    c[:, C_M2:C_M2 + 64] = (s < t)
    c[:, C_M2 + 64:C_M2 + 128] = (s <= t)
    c[:, C_ML:C_ML + 64] = (t < s)
    c[:, C_I64:C_I64 + 64] = (s == t)
    c[:, C_I128:C_I128 + 128] = np.eye(128)
    c[:, C_BO:C_BO + 128] = (p[:, None] // 64 == p[None, :] // 64)
    sm = np.ones(512, np.float32)
    sm[::64] = 0
    c[:, C_SC:C_SC + 512] = sm[None, :]
    return c


def kernel(x, c, w_ada, b_ada, norm_g, w_in, pool_w, pool_scale, mu_shift, w0, w_up, a0, a_up, k_k, k_a, r_k,
           ln_w, ln_b, w_out, final_g):
    f32 = np.float32
    x = np.asarray(x, f32)
    B, T, _ = x.shape
    seg = 1024
    perm_seg = np.concatenate([np.arange(0, 512), np.arange(576, 1088), np.arange(1088, 1600), np.arange(1664, 2176),
                               np.arange(512, 576), np.arange(1600, 1664)])
    perm = np.concatenate([np.arange(0, 1024), seg + perm_seg])
    w_in_p = np.ascontiguousarray(np.asarray(w_in, f32)[0][:, perm])
    mu_p = np.asarray(mu_shift, f32)[0][perm_seg]

    def col(v, n):
        return np.ascontiguousarray(np.asarray(v, f32).reshape(n, 128).T)

    wup = np.concatenate([np.asarray(w_up, f32)[0], np.asarray(a_up, f32)[0]], axis=0)
    cst0 = _consts()
    b_ada0 = np.asarray(b_ada, f32)[0]
    bcast = np.zeros((128, 2, D), f32)
    bcast[:, 0, :] = b_ada0[2 * D:][None, :]
    bcast[:, 1, :] = np.asarray(final_g, f32)[None, :]

    in_maps = []
    for cid in range(8):
        b, j = cid // 4, cid % 4
        t0 = j * NT
        xs = np.zeros((NT + HALO, D), f32)
        xs[HALO:] = x[b, t0:t0 + NT]
        if j > 0:
            xs[:HALO] = x[b, t0 - HALO:t0]
        pvv = np.zeros((128, NPV), f32)
        pvv[:, 0:8] = col(np.asarray(norm_g)[0], 8)
        pvv[:, 8:24] = col(b_ada0[:2 * D], 16)
        pvv[:, 24:28] = col(np.asarray(pool_scale)[0], 4)
        pvv[:, 28:45] = col(mu_p, 17)
        pvv[:, 45:49] = col(np.asarray(w0)[0], 4)
        pvv[:, 49:53] = col(np.asarray(a0)[0], 4)
        pvv[:, 53:57] = col(np.asarray(k_k)[0], 4)
        pvv[:, 57:61] = col(np.asarray(k_a)[0], 4)
        pvv[:, 61:65] = col(np.asarray(r_k)[0], 4)
        pvv[:, 65:69] = col(np.asarray(ln_w)[0], 4)
        pvv[:, 69:73] = col(np.asarray(ln_b)[0], 4)
        pvv[:, 73] = 1.0 if j > 0 else 0.0
        for r in range(8):
            sel = 1.0 if (r // 4 == b and r % 4 < j) else 0.0
            pvv[:, 74 + r] = sel
            pvv[:, 82 + r] = 1.0 - sel
        cst = cst0.copy()
        for g in range(4):
            win = 2 ** (g + 1)
            pos = np.arange(1, 17)
            cnt = np.minimum(pos, win) if j == 0 else np.full(16, win)
            cst[:, C_INV + g * 16:C_INV + (g + 1) * 16] = (1.0 / cnt)[None, :]
        crep = np.ascontiguousarray(np.broadcast_to(np.asarray(c, f32)[b].reshape(8, 128).T[:, :, None], (128, 8, 128)))
        in_maps.append({
            "x": xs, "crep": crep, "w_ada": np.ascontiguousarray(np.asarray(w_ada, f32)[0]), "w_in": w_in_p,
            "w_out": np.ascontiguousarray(np.asarray(w_out, f32)[0]), "pool_w": np.ascontiguousarray(np.asarray(pool_w, f32)[0]),
            "wup": np.ascontiguousarray(wup), "pv": pvv, "bcast": bcast, "cst": cst,
        })
    if "nc" not in _NC_CACHE:
        _NC_CACHE["nc"] = build_nc()
    res = run_bass_kernel_spmd(_NC_CACHE["nc"], in_maps, core_ids=list(range(8)))
    out = np.zeros((B, T, D), f32)
    for cid in range(8):
        b, j = cid // 4, cid % 4
        out[b, j * NT:(j + 1) * NT] = res.results[cid]["out"]
    return out
```
